# Optimizing a Trainium2 kernel written in Bass

```python
import jax, jax.numpy as jnp
from jax import lax
import numpy as np

D_MODEL = 1024
BATCH = 4
SEQ = 4096
DEPTH = 4

N_A_LAYERS = DEPTH // 2
N_B_LAYERS = DEPTH - N_A_LAYERS
INNER_A = 2 * D_MODEL
HEADS_A = 4
HEAD_DIM_A = INNER_A // HEADS_A
QKV_BLOCK = 4
CONV_K = 4
CHUNK = 64
HEAD_DIM_B = 128
HEADS_B = D_MODEL // HEAD_DIM_B
INNER_B = HEADS_B * HEAD_DIM_B
GROUPS_B = ((128, 1), (512, 4), (2048, 16))
N_GROUPS_B = len(GROUPS_B)
BLOCK_B = 128
ALPHA = (2 * DEPTH) ** 0.25
BETA = (8 * DEPTH) ** -0.25
LN_EPS = 1e-5

kernel_name = 'yoco_mlstm_dilated_swa_deepnorm'


def layer_norm(x, g, b):
    xf = x.astype(jnp.float32)
    mu = xf.mean(-1, keepdims=True)
    var = jnp.square(xf - mu).mean(-1, keepdims=True)
    return ((xf - mu) * lax.rsqrt(var + LN_EPS) * g + b).astype(x.dtype)


def causal_dwconv(x, w, b):
    c = x.shape[-1]
    y = lax.conv_general_dilated(x, w[:, None, :], window_strides=(1,), padding=[(CONV_K - 1, 0)],
                                 dimension_numbers=('NWC', 'WIO', 'NWC'), feature_group_count=c)
    return y + b


def headwise(x, w):
    bsz, s = x.shape[:2]
    xr = x.reshape(bsz, s, -1, QKV_BLOCK)
    return jnp.einsum('bsnj,nij->bsni', xr, w).reshape(bsz, s, -1)


def to_chunks(t):
    bsz, s, h = t.shape[:3]
    t = t.astype(jnp.float32).reshape(bsz, s // CHUNK, CHUNK, h, *t.shape[3:])
    perm = (1, 0, 3, 2) + tuple(range(4, t.ndim))
    return t.transpose(perm)


def mlstm_chunkwise(q, k, v, i_pre, f_pre):
    bsz, s, h, dh = q.shape
    log_i = i_pre.astype(jnp.float32)
    log_f = jax.nn.log_sigmoid(f_pre.astype(jnp.float32))
    xs = (to_chunks(q), to_chunks(k) * dh ** -0.5, to_chunks(v), to_chunks(log_i), to_chunks(log_f))
    tri = jnp.tril(jnp.ones((CHUNK, CHUNK), dtype=bool))

    def step(carry, chunk):
        c_mat, n_vec, m = carry
        qc, kc, vc, li, lf = chunk
        b = jnp.cumsum(lf, axis=-1)
        g = b[..., -1]
        d_intra = jnp.where(tri, b[..., :, None] - b[..., None, :] + li[..., None, :], -jnp.inf)
        a_inter = b + m[..., None]
        m_t = jnp.maximum(a_inter, d_intra.max(-1))
        w_intra = jnp.exp(d_intra - m_t[..., None])
        w_inter = jnp.exp(a_inter - m_t)
        sc = jnp.einsum('bhld,bhsd->bhls', qc, kc) * w_intra
        num = w_inter[..., None] * jnp.einsum('bhvk,bhlk->bhlv', c_mat, qc) + jnp.einsum('bhls,bhsv->bhlv', sc, vc)
        den = w_inter * jnp.einsum('bhk,bhlk->bhl', n_vec, qc) + sc.sum(-1)
        h_out = num / jnp.maximum(jnp.abs(den), jnp.exp(-m_t))[..., None]
        log_w = g[..., None] - b + li
        m_new = jnp.maximum(g + m, log_w.max(-1))
        w_s = jnp.exp(log_w - m_new[..., None])
        decay = jnp.exp(g + m - m_new)
        c_mat = decay[..., None, None] * c_mat + jnp.einsum('bhsv,bhsk->bhvk', w_s[..., None] * vc, kc)
        n_vec = decay[..., None] * n_vec + jnp.einsum('bhs,bhsk->bhk', w_s, kc)
        return (c_mat, n_vec, m_new), h_out

    init = (jnp.zeros((bsz, h, dh, dh), jnp.float32), jnp.zeros((bsz, h, dh), jnp.float32),
            jnp.zeros((bsz, h), jnp.float32))
    _, hs = lax.scan(step, init, xs)
    return hs.transpose(1, 0, 3, 2, 4).reshape(bsz, s, h, dh)


def mlstm_layer(x, w_in, conv_w, conv_b, wq, wk, wv, w_if, b_if, gn_g, skip, w_out):
    bsz, s, _ = x.shape
    xm, z, o_pre = jnp.split(x @ w_in, 3, axis=-1)
    xc = jax.nn.silu(causal_dwconv(xm, conv_w, conv_b))
    q = headwise(xc, wq)
    k = headwise(xc, wk)
    v = headwise(xm, wv)
    gates = jnp.concatenate([q, k, v], axis=-1) @ w_if + b_if
    i_pre, f_pre = jnp.split(gates, 2, axis=-1)
    shp = lambda t: t.reshape(bsz, s, HEADS_A, HEAD_DIM_A)
    h_tilde = mlstm_chunkwise(shp(q), shp(k), shp(v), i_pre, f_pre)
    h = jax.nn.sigmoid(shp(o_pre).astype(jnp.float32)) * h_tilde
    mu = h.mean(-1, keepdims=True)
    var = jnp.square(h - mu).mean(-1, keepdims=True)
    h = ((h - mu) * lax.rsqrt(var + LN_EPS)).reshape(bsz, s, INNER_A) * gn_g
    y = (h.astype(x.dtype) + skip * xc) * jax.nn.silu(z)
    return y @ w_out


def to_residue_blocks(t, dilation):
    bsz, s = t.shape[:2]
    rest = t.shape[2:]
    u = s // dilation
    nb = -(-u // BLOCK_B)
    t = t.reshape(bsz, u, dilation, *rest)
    t = jnp.moveaxis(t, 2, 1).reshape(bsz * dilation, u, *rest)
    t = jnp.pad(t, [(0, 0), (0, nb * BLOCK_B - u)] + [(0, 0)] * len(rest))
    return t.reshape(bsz * dilation, nb, BLOCK_B, *rest)


def from_residue_blocks(t, bsz, s, dilation):
    u = s // dilation
    rest = t.shape[3:]
    t = t.reshape(bsz * dilation, -1, *rest)[:, :u]
    t = jnp.moveaxis(t.reshape(bsz, dilation, u, *rest), 1, 2)
    return t.reshape(bsz, s, *rest)


def with_previous_block(t):
    prev = jnp.pad(t[:, :-1], [(0, 0), (1, 0), (0, 0), (0, 0), (0, 0)])
    return jnp.concatenate([prev, t], axis=2)


def alibi_slopes(n_heads):
    return 2.0 ** (-8.0 * (jnp.arange(n_heads, dtype=jnp.float32) + 1.0) / n_heads)


def shared_kv(x, w_kv):
    bsz, s, _ = x.shape
    kv = (x @ w_kv).reshape(bsz, s, 2 * N_GROUPS_B, HEADS_B, HEAD_DIM_B)
    blocks = []
    for g, (window, dilation) in enumerate(GROUPS_B):
        kb = with_previous_block(to_residue_blocks(kv[:, :, 2 * g], dilation))
        vb = with_previous_block(to_residue_blocks(kv[:, :, 2 * g + 1], dilation))
        blocks.append((kb, vb))
    return blocks


def dilated_block_attention(qb, kb, vb, dilation, n_back, slopes):
    nb = qb.shape[1]
    sc = jnp.einsum('nbqhd,nbkhd->nbhqk', qb, kb, preferred_element_type=jnp.float32) * HEAD_DIM_B ** -0.5
    qi = jnp.arange(BLOCK_B)[:, None]
    kj = jnp.arange(2 * BLOCK_B)[None, :]
    j = qi + BLOCK_B - kj
    band = (j >= 0) & (j <= n_back)
    first = (jnp.arange(nb) == 0)[:, None, None] & (kj < BLOCK_B)[None]
    valid = band[None] & ~first
    bias = -slopes[:, None, None] * (j * dilation).astype(jnp.float32)[None]
    sc = jnp.where(valid[None, :, None], sc + bias[None, None], -jnp.inf)
    m = sc.max(-1)
    e = jnp.exp(sc - m[..., None])
    den = e.sum(-1)
    o = jnp.einsum('nbhqk,nbkhd->nbqhd', e, vb.astype(jnp.float32)) / jnp.moveaxis(den, 2, 3)[..., None]
    lse = jnp.moveaxis(m + jnp.log(den), 2, 3)
    return o, lse


def dilated_attention_layer(x, w_in, w_out, kv_blocks, slopes):
    bsz, s, _ = x.shape
    proj = x @ w_in
    q_all = proj[..., :N_GROUPS_B * INNER_B].reshape(bsz, s, N_GROUPS_B, HEADS_B, HEAD_DIM_B)
    z = proj[..., N_GROUPS_B * INNER_B:]
    outs, lses = [], []
    for g, (window, dilation) in enumerate(GROUPS_B):
        qb = to_residue_blocks(q_all[:, :, g], dilation)
        kb, vb = kv_blocks[g]
        o, lse = dilated_block_attention(qb, kb, vb, dilation, window // dilation, slopes)
        outs.append(from_residue_blocks(o, bsz, s, dilation))
        lses.append(from_residue_blocks(lse, bsz, s, dilation))
    wts = jax.nn.softmax(jnp.stack(lses), axis=0)
    o = jnp.sum(wts[..., None] * jnp.stack(outs), axis=0).reshape(bsz, s, INNER_B)
    y = o.astype(x.dtype) * jax.nn.silu(z)
    return y @ w_out


def setup_inputs(seed: int = 0) -> dict:
    key = jax.random.key(seed)
    ks = jax.random.split(key, 20)
    nrm = jax.random.normal
    f32 = jnp.float32
    x = nrm(ks[0], (BATCH, SEQ, D_MODEL), f32)
    ln_g = 1.0 + 0.02 * nrm(ks[1], (DEPTH, D_MODEL), f32)
    ln_b = 0.02 * nrm(ks[2], (DEPTH, D_MODEL), f32)
    a_w_in = nrm(ks[3], (N_A_LAYERS, D_MODEL, 3 * INNER_A), f32) * D_MODEL ** -0.5
    a_conv_w = nrm(ks[4], (N_A_LAYERS, CONV_K, INNER_A), f32) * CONV_K ** -0.5
    a_conv_b = 0.02 * nrm(ks[5], (N_A_LAYERS, INNER_A), f32)
    nblk = INNER_A // QKV_BLOCK
    a_wq = nrm(ks[6], (N_A_LAYERS, nblk, QKV_BLOCK, QKV_BLOCK), f32) * QKV_BLOCK ** -0.5
    a_wk = nrm(ks[7], (N_A_LAYERS, nblk, QKV_BLOCK, QKV_BLOCK), f32) * QKV_BLOCK ** -0.5
    a_wv = nrm(ks[8], (N_A_LAYERS, nblk, QKV_BLOCK, QKV_BLOCK), f32) * QKV_BLOCK ** -0.5
    a_w_if = nrm(ks[9], (N_A_LAYERS, 3 * INNER_A, 2 * HEADS_A), f32) * (3 * INNER_A) ** -0.5
    a_b_if = jnp.concatenate([0.1 * nrm(ks[10], (N_A_LAYERS, HEADS_A), f32),
                              jnp.linspace(3.0, 6.0, HEADS_A, dtype=f32)[None]
                              + 0.1 * nrm(ks[11], (N_A_LAYERS, HEADS_A), f32)], axis=-1)
    a_gn_g = 1.0 + 0.02 * nrm(ks[12], (N_A_LAYERS, INNER_A), f32)
    a_skip = 1.0 + 0.02 * nrm(ks[13], (N_A_LAYERS, INNER_A), f32)
    a_w_out = nrm(ks[14], (N_A_LAYERS, INNER_A, D_MODEL), f32) * INNER_A ** -0.5 * BETA
    b_w_kv = nrm(ks[15], (D_MODEL, 2 * N_GROUPS_B * INNER_B), f32) * D_MODEL ** -0.5
    b_w_in = nrm(ks[16], (N_B_LAYERS, D_MODEL, (N_GROUPS_B + 1) * INNER_B), f32) * D_MODEL ** -0.5
    b_w_out = nrm(ks[17], (N_B_LAYERS, INNER_B, D_MODEL), f32) * INNER_B ** -0.5 * BETA
    return {'x': x, 'ln_g': ln_g, 'ln_b': ln_b, 'a_w_in': a_w_in, 'a_conv_w': a_conv_w, 'a_conv_b': a_conv_b,
            'a_wq': a_wq, 'a_wk': a_wk, 'a_wv': a_wv, 'a_w_if': a_w_if, 'a_b_if': a_b_if, 'a_gn_g': a_gn_g,
            'a_skip': a_skip, 'a_w_out': a_w_out, 'b_w_kv': b_w_kv, 'b_w_in': b_w_in, 'b_w_out': b_w_out}


def reference(x, ln_g, ln_b, a_w_in, a_conv_w, a_conv_b, a_wq, a_wk, a_wv, a_w_if, a_b_if, a_gn_g, a_skip,
              a_w_out, b_w_kv, b_w_in, b_w_out):
    slopes = alibi_slopes(HEADS_B)
    kv_blocks = None
    for layer in range(DEPTH):
        if layer < N_A_LAYERS:
            y = mlstm_layer(x, a_w_in[layer], a_conv_w[layer], a_conv_b[layer], a_wq[layer], a_wk[layer],
                            a_wv[layer], a_w_if[layer], a_b_if[layer], a_gn_g[layer], a_skip[layer],
                            a_w_out[layer])
        else:
            lb = layer - N_A_LAYERS
            y = dilated_attention_layer(x, b_w_in[lb], b_w_out[lb], kv_blocks, slopes)
        x = layer_norm(ALPHA * x + y, ln_g[layer], ln_b[layer])
        if layer == N_A_LAYERS - 1:
            kv_blocks = shared_kv(x, b_w_kv)
    return x
```

```python
import concourse.bass as bass
import concourse.mybir as mybir


PSUM_KEYS = {"pA", "pB", "pO", "pN", "pU0", "pS", "pBu", "pBm"}


class Op:
    __slots__ = ("eng", "fn", "deps", "signal", "cnt", "is_dma", "dsem", "idx")

    def __init__(self, eng, fn, is_dma=False, dsem=None):
        self.eng = eng
        self.fn = fn
        self.deps = set()
        self.signal = False
        self.cnt = None
        self.is_dma = is_dma
        self.dsem = dsem
        self.idx = None


class Sched:
    ENGS = ("pe", "act", "dve", "pool", "sp")

    def __init__(self):
        self.ops = []
        self.last_w = {}
        self.readers = {}

    def barrier(self, fn):
        keys = list(self.last_w.keys()) + list(self.readers.keys())
        return self.add("dve", fn, reads=(), writes=list(dict.fromkeys(keys + ["__epoch__"])))

    def add(self, eng, fn, reads=(), writes=(), dma=None, r=None, w=None):
        if r is not None:
            reads = r
        if w is not None:
            writes = w
        pr = [k for k in reads if k in PSUM_KEYS]
        if pr:
            writes = list(writes) + [k for k in pr if k not in writes]
            reads = [k for k in reads if k not in PSUM_KEYS]
        reads = list(reads) + ["__epoch__"]
        op = Op(eng, fn, is_dma=dma is not None, dsem=dma)
        op.idx = len(self.ops)
        deps = set()
        for r in reads:
            w = self.last_w.get(r)
            if w is not None:
                deps.add(w)
        for w_ in writes:
            w = self.last_w.get(w_)
            if w is not None:
                deps.add(w)
            for rd in self.readers.get(w_, ()):
                deps.add(rd)
        deps.discard(op.idx)
        fdeps = set()
        for d in deps:
            dop = self.ops[d]
            if dop.is_dma:
                fdeps.add(d)
                continue
            if op.is_dma:
                fdeps.add(d)
                continue
            if dop.eng == eng:
                if eng == "pe":
                    continue
                israw = False
                for r in reads:
                    if self.last_w.get(r) == d:
                        israw = True
                        break
                if not israw:
                    continue
            fdeps.add(d)
        op.deps = fdeps
        self.ops.append(op)
        for r in reads:
            self.readers.setdefault(r, []).append(op.idx)
        for w_ in writes:
            self.last_w[w_] = op.idx
            self.readers[w_] = []
        return op

    def emit(self, nc, stack, final_wait_ops=()):
        ops = self.ops
        for op in ops:
            for d in op.deps:
                ops[d].signal = True
        for op in final_wait_ops:
            op.signal = True
        esem = {e: stack.enter_context(nc.semaphore("s_" + e)) for e in self.ENGS}
        dsems = {}
        for op in ops:
            if op.is_dma and op.dsem not in dsems:
                dsems[op.dsem] = stack.enter_context(nc.semaphore("d_%s" % (op.dsem,)))
        ecnt = {e: 0 for e in self.ENGS}
        dcnt = {k: 0 for k in dsems}
        dma_cum_at = []
        dlist = {k: [] for k in dsems}
        for op in ops:
            if op.is_dma:
                dcnt[op.dsem] += 16
                op.cnt = dcnt[op.dsem]
                dlist[op.dsem].append(op.idx)
            elif op.signal:
                ecnt[op.eng] += 1
                op.cnt = ecnt[op.eng]
        import bisect
        per_eng = {e: [] for e in self.ENGS}
        seen = {e: {} for e in self.ENGS}
        for op in ops:
            waits = {}
            for d in op.deps:
                dop = ops[d]
                if dop.is_dma:
                    lst = dlist[dop.dsem]
                    j = bisect.bisect_left(lst, op.idx) - 1
                    val = ops[lst[j]].cnt
                    key = ("d", dop.dsem)
                else:
                    val = dop.cnt
                    key = ("e", dop.eng)
                if val > waits.get(key, 0):
                    waits[key] = val
            wl = []
            sn = seen[op.eng]
            for key, val in waits.items():
                if sn.get(key, 0) >= val:
                    continue
                sn[key] = val
                sem = dsems[key[1]] if key[0] == "d" else esem[key[1]]
                wl.append((sem, val))
            per_eng[op.eng].append((op, wl))
        self.esem, self.dsems = esem, dsems
        finals = []
        for op in final_wait_ops:
            if op.is_dma:
                finals.append((dsems[op.dsem], dcnt[op.dsem]))
            else:
                finals.append((esem[op.eng], op.cnt))

        def run(engname, eng):
            for op, wl in per_eng[engname]:
                for sem, val in wl:
                    eng.wait_ge(sem, val)
                ins = op.fn(eng)
                if op.is_dma:
                    ins.then_inc(dsems[op.dsem], 16)
                elif op.signal:
                    ins.then_inc(esem[op.eng], 1)

        with nc.Block() as block:
            @block.tensor
            def _(e):
                run("pe", e)

            @block.scalar
            def _(e):
                run("act", e)

            @block.vector
            def _(e):
                run("dve", e)

            @block.gpsimd
            def _(e):
                run("pool", e)

            @block.sync
            def _(e):
                run("sp", e)
                for sem, val in finals:
                    e.wait_ge(sem, val)
        return {e: len(per_eng[e]) for e in self.ENGS}


import math
import numpy as np
from contextlib import ExitStack
from concourse.bass_utils import run_bass_kernel_spmd


F32 = mybir.dt.float32
BF16 = mybir.dt.bfloat16
AF = mybir.ActivationFunctionType
ALU = mybir.AluOpType

S = 4096
D = 1024
NCH = D // 128
INNER = 2048
NIC = INNER // 128
HA = 4
DH = 512
T = 128
NT = S // T
DEPTH = 4
ALPHA = (2 * DEPTH) ** 0.25
LN_EPS = 1e-5
LNK = math.log(DH ** -0.5)


def build(n_layers=4, n_tiles=NT):
    nc = bass.Bass("TRN2", target_bir_lowering=False)

    def din(name, shape, dt=F32):
        return nc.dram_tensor(name, list(shape), dt, kind="ExternalInput").ap()

    x_d = din("x", [S, D])
    out_d = nc.dram_tensor("out", [S, D], F32, kind="ExternalOutput").ap()
    cst_d = din("cst", [128, 6, 128])
    a_win_d = [din("a_win%d" % l, [D, 3 * INNER]) for l in range(2)]
    a_wout_d = [din("a_wout%d" % l, [INNER, D]) for l in range(2)]
    a_bd_d = [din("a_bd%d" % l, [128, 3 * NIC * 128]) for l in range(2)]
    a_wif_d = [din("a_wif%d" % l, [128, 48 * 8]) for l in range(2)]
    a_vec_d = [din("a_vec%d" % l, [128, 8 + 64 + 16 * 3]) for l in range(2)]
    ln_d = din("lnv", [128, DEPTH * 2 * NCH])
    a_win_b = [nc.dram_tensor("a_winb%d" % l, [D, 3 * INNER], BF16, kind="Internal").ap() for l in range(2)]
    a_wout_b = [nc.dram_tensor("a_woutb%d" % l, [INNER, D], BF16, kind="Internal").ap() for l in range(2)]

    jm_d = din("jm", [128, 512])
    b_wkv_d = din("b_wkv", [D, 6 * D])
    b_win_d = [din("b_win%d" % l, [D, 4 * D]) for l in range(2)]
    b_wout_d = [din("b_wout%d" % l, [D, D]) for l in range(2)]
    b_wkv_b = nc.dram_tensor("b_wkvb", [D, 6 * D], BF16, kind="Internal").ap()
    b_win_b = [nc.dram_tensor("b_winb%d" % l, [D, 4 * D], BF16, kind="Internal").ap() for l in range(2)]
    b_wout_b = [nc.dram_tensor("b_woutb%d" % l, [D, D], BF16, kind="Internal").ap() for l in range(2)]
    kt_d = nc.dram_tensor("kt_d", [24, 128, S], BF16, kind="Internal").ap()
    v_d = nc.dram_tensor("v_d", [3, S, D], BF16, kind="Internal").ap()
    y_d = nc.dram_tensor("y_d", [8, 128, S], BF16, kind="Internal").ap()

    sc = Sched()
    with ExitStack() as st:
        def sb(name, shape, dt):
            return st.enter_context(nc.sbuf_tensor(name, list(shape), dt))

        def ps(name, shape, dt):
            return st.enter_context(nc.psum_tensor(name, list(shape), dt))

        def MM(out, lhsT, rhs, start=True, stop=True, r=(), w=()):
            return sc.add("pe", lambda e: e.matmul(out, lhsT=lhsT, rhs=rhs, start=start, stop=stop), r, w)

        def TR(out, in_, ident, r=(), w=()):
            return sc.add("pe", lambda e: e.transpose(out=out, in_=in_, identity=ident), r, w)

        def ACT(out, in_, func, bias=None, scale=None, r=(), w=()):
            kw = {}
            if bias is not None:
                kw["bias"] = bias
            if scale is not None:
                kw["scale"] = scale
            return sc.add("act", lambda e: e.activation(out=out, in_=in_, func=func, **kw), r, w)

        def CP(eng, out, in_, r=(), w=()):
            if eng == "act":
                return sc.add("act", lambda e: e.copy(out=out, in_=in_), r, w)
            return sc.add(eng, lambda e: e.tensor_copy(out=out, in_=in_), r, w)

        def TT(eng, out, in0, in1, op, r=(), w=()):
            return sc.add(eng, lambda e: e.tensor_tensor(out=out, in0=in0, in1=in1, op=op), r, w)

        def TS(eng, out, in0, s1, s2, op0, op1=None, r=(), w=()):
            if op1 is None:
                return sc.add(eng, lambda e: e.tensor_scalar(out=out, in0=in0, scalar1=s1, scalar2=None, op0=op0), r, w)
            return sc.add(eng, lambda e: e.tensor_scalar(out=out, in0=in0, scalar1=s1, scalar2=s2, op0=op0, op1=op1), r, w)

        def STT(eng, out, in0, scalar, in1, op0, op1, r=(), w=()):
            return sc.add(eng, lambda e: e.scalar_tensor_tensor(out=out, in0=in0, scalar=scalar, in1=in1, op0=op0, op1=op1), r, w)

        pl_cnt = [0]

        def PRELOAD(func):
            pl_cnt[0] += 1
            return sc.add("act", lambda e: e.activation(out=sm[:, 11:12], in_=sm[:, 10:11], func=func),
                          ["sm_dummy"], [("sm_dummy_out", pl_cnt[0])])

        def DMA(q, out, in_, sem, r=(), w=()):
            return sc.add(q, lambda e: e.dma_start(out=out, in_=in_), r, w, dma=sem)

        xT = sb("xT", [128, NCH, S], BF16)
        cst = sb("cst_sb", [128, 6, 128], F32)
        identf, utri, mneg, onesf = cst[:, 0, :], cst[:, 1, :], cst[:, 2, :], cst[:, 3, :]
        identb = sb("identb", [128, 128], BF16)
        onesb = sb("onesb", [128, 2], BF16)
        epsb = sb("epsb", [128, 1], F32)
        mneg4 = sb("mneg4", [128, 4, 128], F32)
        lnv = sb("lnv_sb", [128, DEPTH, 2, NCH], F32)

        pA = ps("pA", [128, 512], F32)
        pB = ps("pB", [128, 512], F32)
        pO = ps("pO", [128, 512], F32)
        pN = ps("pN", [128, 512], F32)
        pU = [ps("pU0", [128, 512], F32), ps("pU1", [128, 512], F32)]
        pBu = ps("pBu", [128, 512], F32)
        pBm = ps("pBm", [128, 512], F32)

        DMA("sp", cst[:], cst_d[:, :, :], "const", w=["cst"])
        DMA("sp", lnv[:].rearrange("p a b c -> p (a b c)"), ln_d[:, :], "const", w=["lnv"])
        CP("dve", identb[:], identf, r=["cst"], w=["identb"])
        sc.add("dve", lambda e: e.memset(onesb[:], 1.0), w=["onesb"])
        sc.add("dve", lambda e: e.memset(epsb[:], LN_EPS), w=["epsb"])
        for h in range(4):
            CP("pool", mneg4[:, h, :], mneg, r=["cst"], w=["mneg4"])

        nl_a = min(n_layers, 2)
        ring = [sb("ring%d" % i, [128, 4096], BF16) for i in range(3)]
        cv = [0]

        def convert_block(src2d, dst2d, rows, c0, c1, key, sem):
            for r0 in range(0, rows, 1024):
                DMA("pool", dst2d[r0:r0 + 1024, c0:c1], src2d[r0:r0 + 1024, c0:c1], sem, w=[key])

        for l in range(nl_a):
            for i in range(4):
                convert_block(a_win_d[l], a_win_b[l], D, i * 512, (i + 1) * 512, ("winb", l, i), "cv_a%d_m" % l)
            for h_ in range(4):
                for base in (8, 4):
                    i = base + h_
                    convert_block(a_win_d[l], a_win_b[l], D, i * 512, (i + 1) * 512, ("winb", l, i), "cv_a%d_oz" % l)
            for j in range(4):
                convert_block(a_wout_d[l], a_wout_b[l], INNER, j * 256, (j + 1) * 256, ("woutb", l, j), "cv_a%d_w" % l)
        if n_layers > 2:
            for j in range(12):
                convert_block(b_wkv_d, b_wkv_b, D, j * 512, (j + 1) * 512, ("wkvb", j), "cv_kv")
            for lb_ in range(n_layers - 2):
                for j in range(8):
                    convert_block(b_win_d[lb_], b_win_b[lb_], D, j * 512, (j + 1) * 512, ("bwinb", lb_, j), "cv_b%d" % lb_)
                for j in range(2):
                    convert_block(b_wout_d[lb_], b_wout_b[lb_], D, j * 512, (j + 1) * 512, ("bwoutb", lb_, j), "cv_b%d" % lb_)

        vbuf = sb("vbuf", [128, NCH, T], F32)
        sq = sb("sq", [128, NCH, T], F32)
        stage = [vbuf[:].rearrange("p c t -> p (c t)"), sq[:].rearrange("p c t -> p (c t)")]
        for g in range(S // 128):
            b = g % 2
            DMA("sp", stage[b], x_d[g * 128:(g + 1) * 128, :], ("stage", b), w=[("stage", b)])
            for hlf in range(2):
                pb, pk = (pA, "pA") if hlf == 0 else (pB, "pB")
                for j in range(4):
                    c = hlf * 4 + j
                    TR(pb[:, j * 128:(j + 1) * 128], stage[b][:, c * 128:(c + 1) * 128], identf,
                       r=[("stage", b), "cst"], w=[pk])
                CP("act" if hlf == 0 else "dve", xT[:, hlf * 4:(hlf + 1) * 4, g * 128:(g + 1) * 128],
                   pb[:].rearrange("p (c t) -> p c t", c=4), r=[pk], w=[("xT", g)])

        def ln_tail(t, l, last_layer, vb=None, sqb=None, vk="vbuf", sk="sq"):
            vb = vbuf if vb is None else vb
            sqb = sq if sqb is None else sqb
            tok = slice(t * T, (t + 1) * T)
            xk = ("xT", t)
            TT("dve", sqb[:], vb[:], vb[:], ALU.mult, r=[vk], w=[sk])
            for dc in range(NCH):
                MM(pBu[:, 0:T], onesf, vb[:, dc, :], start=(dc == 0), stop=(dc == NCH - 1), r=["cst", vk], w=["pBu"])
            for dc in range(NCH):
                MM(pBm[:, 0:T], onesf, sqb[:, dc, :], start=(dc == 0), stop=(dc == NCH - 1), r=["cst", sk], w=["pBm"])
            ACT(lnt[:, 0, :], pBu[:, 0:T], AF.Copy, scale=1.0 / D, r=["pBu"], w=["ln_mean", "Abc"])
            TT("pool", lnt[:, 1, :], lnt[:, 0, :], lnt[:, 0, :], ALU.mult, r=["ln_mean"], w=["ln_msq", "Abc"])
            STT("dve", lnt[:, 1, :], pBm[:, 0:T], 1.0 / D, lnt[:, 1, :], ALU.mult, ALU.subtract,
                r=["pBm", "ln_msq"], w=["ln_msq", "Abc"])
            ACT(lnt[:, 2, :], lnt[:, 1, :], AF.Sqrt, bias=epsb[:, 0:1], r=["ln_msq", "epsb"], w=["ln_rstd", "Abc"])
            sc.add("dve", lambda e: e.reciprocal(out=lnt[:, 2, :], in_=lnt[:, 2, :]), r=["ln_rstd"], w=["ln_rstd", "Abc"])
            TT("dve", vb[:], vb[:], lnt[:, 0, :].unsqueeze(1).to_broadcast([128, NCH, T]), ALU.subtract,
               r=[vk, "ln_mean", sk], w=[vk])
            TT("dve", vb[:], vb[:], lnt[:, 2, :].unsqueeze(1).to_broadcast([128, NCH, T]), ALU.mult,
               r=[vk, "ln_rstd"], w=[vk])
            for dc in range(NCH):
                if not last_layer:
                    ACT(xT[:, dc, tok], vb[:, dc, :], AF.Identity, bias=lnv[:, l, 1, dc:dc + 1],
                        scale=lnv[:, l, 0, dc:dc + 1], r=[vk, "lnv"], w=[xk])
                else:
                    ACT(sqb[:, dc, :], vb[:, dc, :], AF.Identity, bias=lnv[:, l, 1, dc:dc + 1],
                        scale=lnv[:, l, 0, dc:dc + 1], r=[vk, "lnv"], w=[sk])
            if last_layer:
                if l >= 2:
                    par = t % 2
                    ost = xc32[:].rearrange("p c t -> p (c t)")[:, par * D:(par + 1) * D]
                    oks = [("ost", par, 0), ("ost", par, 1)]
                else:
                    ost = ostage
                    oks = [[("xc32", i_) for i_ in range(8)]] * 2
                for hlf in range(2):
                    pb, pk = (pA, "pA") if hlf == 0 else (pB, "pB")
                    for j in range(4):
                        TR(pb[:, j * 128:(j + 1) * 128], sqb[:, hlf * 4 + j, :], identf, r=[sk, "cst"], w=[pk])
                    wk = [oks[hlf]] if l >= 2 else oks[hlf]
                    CP("act" if hlf == 0 else "dve", ost[:, hlf * 512:(hlf + 1) * 512], pb[:], r=[pk], w=wk)
                rk = oks if l >= 2 else oks[0]
                final_ops.append(DMA("sp", out_d[t * T:(t + 1) * T, :], ost, "ostage", r=rk))

        if nl_a > 0:
            CT = sb("CT", [128, HA, 4, 512], F32)
            CTb0 = sb("CTb0", [128, 4, 512], BF16)
            CTb = [CTb0, CTb0]
            nst = sb("nst", [128, HA, 4], F32)
            nbb = sb("nbb", [128, HA, 4, 2], BF16)
            bd = sb("bd", [128, 3, NIC, 128], BF16)
            wif = sb("wif", [128, 48, 8], BF16)
            avec = sb("avec", [128, 8 + 64 + 48], F32)
            bif = avec[:, 0:8]
            cw = avec[:, 8:72].rearrange("p (c j) -> p c j", j=4)
            cb = avec[:, 72:88]
            gng = avec[:, 88:104]
            skp = avec[:, 104:120]
            xm32 = sb("xm32", [128, NIC, T + 3], F32)
            xmcb = sb("xmcb", [128, 2 * NIC, T], BF16)
            xmb = xmcb[:, 0:NIC, :]
            xcb = xmcb[:, NIC:2 * NIC, :]
            xc32 = sb("xc32", [128, NIC, T], F32)
            qkT = sb("qkT", [128, 2 * NIC, T], BF16)
            qT = qkT[:, 0:NIC, :]
            kT = qkT[:, NIC:2 * NIC, :]
            ktvt = sb("ktvt", [128, 2 * INNER], BF16)
            ktok = ktvt[:, 0:INNER]
            vtok = ktvt[:, INNER:2 * INNER]
            gsb = sb("gsb", [128, 8], F32)
            gtmp = sb("gtmp", [128, 4], F32)
            lf = sb("lf", [128, 4], F32)
            bias_s = sb("bias_s", [128, 4], F32)
            Abc = sb("Abc", [128, 512], F32)
            WT = [sb("WT%d" % i, [128, 128], F32) for i in range(2)]
            scW = [sb("scW%d" % i, [128, 128], BF16) for i in range(2)]
            qs = [sb("qs%d" % i, [128, 4, 128], BF16) for i in range(2)]
            kw = [sb("kw%d" % i, [128, 512], BF16) for i in range(2)]
            sigo = sb("sigo", [128, 512], F32)
            hg = sb("hg", [128, 512], F32)
            rhsB = hg[:].rearrange("p (h l) -> p h l", h=4)
            tmpy = hg[:].rearrange("p (h l) -> p h l", h=4)
            hn = sb("hn", [128, 512], BF16)
            sz = sigo[:].rearrange("p (h l) -> p h l", h=4)
            yT = sb("yT", [128, NIC, T], BF16)
            vT = yT
            lnt = Abc[:].rearrange("p (h l) -> p h l", h=4)
            ptmp = sq[:, 0, :]
            sm = sb("sm", [128, 16], F32)
            bnst = sb("bnst", [128, 6], F32)
            ostage = xc32[:].rearrange("p c t -> p (c t)")[:, 0:D]
            pS = pU[1]
            p_scT = pS[:, 0:128]
            p_den = pS[:, 128:130]
            p_G = pS[:, 136:144]
            p_btok = pS[:, 144:148]
            p_nup = pS[:, 152:160]
            p_hT = pS[:, 256:512].bitcast(BF16)

        for l in range(nl_a):
            last_layer = (l == n_layers - 1)
            DMA("pool", bd[:].rearrange("p a c m -> p (a c m)"), a_bd_d[l][:, :], "lw", w=["bd"])
            DMA("pool", wif[:].rearrange("p c g -> p (c g)"), a_wif_d[l][:, :], "lw", w=["wif"])
            DMA("sp", avec[:], a_vec_d[l][:, :], "lw2", w=["avec"])
            sc.add("dve", lambda e: e.memset(CT[:].rearrange("p h k v -> p (h k v)"), 0.0), w=[("CT", h_) for h_ in range(HA)])
            sc.add("dve", lambda e: e.memset(nst[:].rearrange("p h k -> p (h k)"), 0.0), w=["nst"])
            sc.add("dve", lambda e: e.memset(sm[:, 10:12], 1.0), w=["sm_dummy"])
            sc.add("pool", lambda e: e.memset(nbb[:].rearrange("p h k t -> p (h k t)"), 0.0), w=["nbb"])
            sc.add("pool", lambda e: e.memset(xm32[:, :, 0:3], 0.0), w=[("xm32", i_) for i_ in range(4)])

            winv = a_win_b[l].rearrange("(kc p) c -> p kc c", p=128)
            woutv = a_wout_b[l].rearrange("(cc p) d -> p cc d", p=128)
            blocks = []
            for t in range(n_tiles):
                for i in range(4):
                    blocks.append(("m", i))
                for h in range(HA):
                    blocks.append(("o", h))
                    blocks.append(("z", h))
                for j in range(4):
                    blocks.append(("w", j))
            state = {"issued": 0, "next": 0}

            def issue_block(n, l=l, blocks=blocks, winv=winv, woutv=woutv):
                kind, i = blocks[n]
                slot = n % 3
                if kind == "m":
                    src = winv[:, :, i * 512:(i + 1) * 512]
                    dst = ring[slot][:].rearrange("p (k c) -> p k c", k=8)
                    rk = ("winb", l, i)
                elif kind == "z":
                    src = winv[:, :, INNER + i * 512:INNER + (i + 1) * 512]
                    dst = ring[slot][:].rearrange("p (k c) -> p k c", k=8)
                    rk = ("winb", l, 4 + i)
                elif kind == "o":
                    src = winv[:, :, 2 * INNER + i * 512:2 * INNER + (i + 1) * 512]
                    dst = ring[slot][:].rearrange("p (k c) -> p k c", k=8)
                    rk = ("winb", l, 8 + i)
                else:
                    src = woutv[:, :, i * 256:(i + 1) * 256]
                    dst = ring[slot][:].rearrange("p (k c) -> p k c", k=16)
                    rk = ("woutb", l, i)
                DMA("sp", dst, src, ("ring", slot), r=[rk], w=[("ring", slot)])

            def next_block(state=state, blocks=blocks, issue_block=issue_block):
                n = state["next"]
                while state["issued"] < min(n + 3, len(blocks)):
                    issue_block(state["issued"])
                    state["issued"] += 1
                state["next"] = n + 1
                slot = n % 3
                return ring[slot], ("ring", slot)

            hslot = [0]
            for t in range(n_tiles):
                tok = slice(t * T, (t + 1) * T)
                xk = ("xT", t)
                for blk in range(4):
                    W, wk_ = next_block()
                    Wv = W[:].rearrange("p (k c) -> p k c", k=8)
                    pb, pk = (pA, "pA") if blk % 2 == 0 else (pB, "pB")
                    for j in range(4):
                        for kc in range(8):
                            MM(pb[:, j * 128:(j + 1) * 128], Wv[:, kc, j * 128:(j + 1) * 128], xT[:, kc, tok],
                               start=(kc == 0), stop=(kc == 7), r=[wk_, xk], w=[pk])
                    CP("act", xm32[:, blk * 4:(blk + 1) * 4, 3:3 + T], pb[:].rearrange("p (c t) -> p c t", c=4),
                       r=[pk], w=[("xm32", blk)])
                if DBG == 1:
                    continue
                hgv = hg[:].rearrange("p (h l) -> p h l", h=4)
                for oc in [3, 7, 11, 15] + [o_ for o_ in range(NIC) if o_ % 4 != 3]:
                    if oc % 4 != 3:
                        TS("dve", xc32[:, oc, :], xm32[:, oc, 0:T], cw[:, oc, 0:1], None, ALU.mult,
                           r=[("xm32", oc // 4), "avec"], w=[("xc32", oc)])
                        for j in range(1, 4):
                            STT("dve", xc32[:, oc, :], xm32[:, oc, j:j + T], cw[:, oc, j:j + 1], xc32[:, oc, :],
                                ALU.mult, ALU.add, r=[("xm32", oc // 4), "avec", ("xc32", oc)], w=[("xc32", oc)])
                    else:
                        ACT(xc32[:, oc, :], xm32[:, oc, 0:T], AF.Copy, scale=cw[:, oc, 0:1],
                            r=[("xm32", oc // 4), "avec"], w=[("xc32", oc)])
                        for j in range(1, 4):
                            sl = hslot[0] % 4
                            hslot[0] += 1
                            ACT(hgv[:, sl, :], xm32[:, oc, j:j + T], AF.Copy, scale=cw[:, oc, j:j + 1],
                                r=[("xm32", oc // 4), "avec", "hg"], w=[("hgs", sl)])
                            TT("pool", xc32[:, oc, :], xc32[:, oc, :], hgv[:, sl, :], ALU.add,
                               r=[("hgs", sl), ("xc32", oc), "hg"], w=[("xc32", oc)])
                    ACT(xc32[:, oc, :], xc32[:, oc, :], AF.Silu, bias=cb[:, oc:oc + 1],
                        r=[("xc32", oc), "avec"], w=[("xc32", oc)])
                    if oc % 4 == 2:
                        g_ = oc // 4
                        CP("act", xcb[:, 4 * g_:4 * g_ + 4, :], xc32[:, 4 * g_:4 * g_ + 4, :],
                           r=[("xc32", 4 * g_ + i_) for i_ in range(4)], w=[("xcb", g_)])
                allxc = [("xc32", oc) for oc in range(NIC)]
                allxm = [("xm32", i_) for i_ in range(4)]
                allxcb = [("xcb", i_) for i_ in range(4)]
                CP("act", xmb, xm32[:, :, 3:3 + T], r=allxm, w=["xmb"])
                CP("pool", xm32[:, :, 0:3], xm32[:, :, T:T + 3], r=allxm, w=allxm)
                TT("dve", xc32[:], xc32[:], skp.unsqueeze(2).to_broadcast([128, NIC, T]), ALU.mult,
                   r=allxc + ["avec"] + allxcb, w=allxc)
                if DBG == 2:
                    continue
                ev = 0
                p3banks = ((pA, "pA"), (pB, "pB"), (pO, "pO"), (pN, "pN"))
                for which, srcb, srck, dst, dk in ((0, xcb, "xcb", qT, "qT"), (1, xcb, "xcb", kT, "kT"),
                                                   (2, xmb, "xmb", vT, ("yT", 0))):
                    for g4 in range(4):
                        pb, pk = p3banks[ev % 4]
                        for j in range(4):
                            oc = g4 * 4 + j
                            MM(pb[:, j * 128:(j + 1) * 128], bd[:, which, oc, :], srcb[:, oc, :],
                               r=["bd", (srck, g4) if srck == "xcb" else srck], w=[pk])
                        if which == 2:
                            CP("dve", dst[:, g4 * 4:(g4 + 1) * 4, :],
                               pb[:].rearrange("p (c t) -> p c t", c=4), r=[pk], w=[dk])
                        else:
                            CP("act" if ev % 2 == 0 else "dve", dst[:, g4 * 4:(g4 + 1) * 4, :],
                               pb[:].rearrange("p (c t) -> p c t", c=4), r=[pk], w=[(dk, g4)])
                        ev += 1
                if DBG == 3:
                    continue
                for i, (srcb, srck) in enumerate(((qT, "qT"), (kT, "kT"), (vT, ("yT", 0)))):
                    for oc in range(NIC):
                        cc = i * NIC + oc
                        MM(p_G, srcb[:, oc, :], wif[:, cc, :], start=(cc == 0), stop=(cc == 47),
                           r=[(srck, oc // 4) if i < 2 else srck, "wif"], w=["pS"])
                TT("dve", gsb[:], p_G, bif, ALU.add, r=["pS", "avec"], w=["gsb"])
                ACT(gtmp[:], gsb[:, 4:8], AF.Exp, scale=-1.0, r=["gsb"], w=["gtmp"])
                ACT(gtmp[:], gtmp[:], AF.Ln, bias=1.0, r=["gtmp"], w=["gtmp"])
                TS("dve", lf[:], gtmp[:], -1.0, None, ALU.mult, r=["gtmp"], w=["lf"])
                TT("pool", rhsB, utri.unsqueeze(1).to_broadcast([128, 4, 128]),
                   lf[:].unsqueeze(2).to_broadcast([128, 4, 128]), ALU.mult, r=["cst", "lf"],
                   w=["hg"] + [("hgs", i_) for i_ in range(4)])
                for which, srcb, srck, dst, dk in ((1, xcb, "xcb", ktok, "ktok"), (2, xmb, "xmb", vtok, "vtok")):
                    for g4 in range(4):
                        pb, pk = p3banks[ev % 4]
                        for j in range(4):
                            oc = g4 * 4 + j
                            MM(pb[:, j * 128:(j + 1) * 128], srcb[:, oc, :], bd[:, which, oc, :],
                               r=["bd", (srck, g4) if srck == "xcb" else srck], w=[pk])
                        CP("act" if ev % 2 == 0 else "dve", dst[:, g4 * 512:(g4 + 1) * 512], pb[:], r=[pk], w=[(dk, g4)])
                        ev += 1
                rB = hg[:]
                MM(pBu[:], onesf, rB, r=["cst", "hg"], w=["pBu"])
                MM(pBm[:], onesf, rB, start=True, stop=False, r=["cst", "hg"], w=["pBm"])
                MM(pBm[:], identf, mneg4[:].rearrange("p h l -> p (h l)"), start=False, stop=True,
                   r=["cst", "mneg4"], w=["pBm"])
                MM(p_btok, utri, lf[:], r=["cst", "lf"], w=["pS"])
                STT("dve", bias_s[:], p_btok, -1.0, gsb[:, 0:4], ALU.mult, ALU.add, r=["pS", "gsb"], w=["bias_s"])
                TS("dve", bias_s[:], bias_s[:], LNK, None, ALU.add, r=["bias_s"], w=["bias_s"])
                ACT(Abc[:], pBu[:], AF.Exp, r=["pBu"], w=["Abc"])
                if DBG == 4:
                    continue
                def prep(h):
                    hb = h % 2
                    hs = slice(h * 128, (h + 1) * 128)
                    hv = slice(h * 512, (h + 1) * 512)
                    for dc in range(4):
                        MM(p_scT, kT[:, 4 * h + dc, :], qT[:, 4 * h + dc, :], start=(dc == 0), stop=(dc == 3),
                           r=[("kT", h), ("qT", h)], w=["pS"])
                    ACT(WT[hb][:], pBm[:, hs], AF.Exp, bias=bias_s[:, h:h + 1], r=["pBm", "bias_s"], w=[("WT", hb)])
                    TT("dve", scW[hb][:], p_scT, WT[hb][:], ALU.mult, r=["pS", ("WT", hb)], w=[("scW", hb)])
                    TT("pool", qs[hb][:], qT[:, 4 * h:4 * h + 4, :],
                       Abc[:, hs].unsqueeze(1).to_broadcast([128, 4, 128]), ALU.mult, r=[("qT", h), "Abc"], w=[("qs", hb)])
                    CP("act", CTb[hb][:].rearrange("p k v -> p (k v)"), CT[:, h, :, :].rearrange("p k v -> p (k v)"),
                       r=[("CT", h)], w=[("CTb", 0)])
                    PRELOAD(AF.Sigmoid)
                    TS("dve", kw[hb][:], ktok[:, hv], WT[hb][:, 127:128], None, ALU.mult,
                       r=[("ktok", h), ("WT", hb)], w=[("kw", hb)])

                def proj(h):
                    Wo, wok = next_block()
                    Wov = Wo[:].rearrange("p (k c) -> p k c", k=8)
                    for kc in range(8):
                        MM(pO[:], xT[:, kc, tok], Wov[:, kc, :], start=(kc == 0), stop=(kc == 7), r=[wok, xk], w=["pO"])
                    Wz, wzk = next_block()
                    Wzv = Wz[:].rearrange("p (k c) -> p k c", k=8)
                    pb, pk = (pA, "pA") if h % 2 == 0 else (pB, "pB")
                    for cc in range(4):
                        for kc in range(8):
                            MM(pb[:, cc * 128:(cc + 1) * 128], Wzv[:, kc, cc * 128:(cc + 1) * 128], xT[:, kc, tok],
                               start=(kc == 0), stop=(kc == 7), r=[wzk, xk], w=[pk])

                def scan(h):
                    hb = h % 2
                    hv = slice(h * 512, (h + 1) * 512)
                    for kc in range(4):
                        MM(pN[:], qs[hb][:, kc, :], CTb[hb][:, kc, :], start=(kc == 0), stop=False,
                           r=[("qs", hb), ("CTb", 0)], w=["pN"])
                    MM(pN[:], scW[hb][:], vtok[:, hv], start=False, stop=True, r=[("scW", hb), ("vtok", h)], w=["pN"])
                    for kc in range(4):
                        MM(p_den, qs[hb][:, kc, :], nbb[:, h, kc, :], start=(kc == 0), stop=False,
                           r=[("qs", hb), "nbb"], w=["pS"])
                    MM(p_den, scW[hb][:], onesb[:], start=False, stop=True, r=[("scW", hb), "onesb"], w=["pS"])
                    for kc in range(4):
                        MM(p_nup[:, 2 * kc:2 * kc + 2], kw[hb][:, kc * 128:(kc + 1) * 128], onesb[:],
                           r=[("kw", hb), "onesb"], w=["pS"])
                    decay = Abc[:, h * 128 + 127:h * 128 + 128]
                    for kc in range(4):
                        pu, puk = (pU[0], "pU0") if kc % 2 == 0 else (pBu, "pBu")
                        MM(pu[:], kw[hb][:, kc * 128:(kc + 1) * 128], vtok[:, hv], r=[("kw", hb), ("vtok", h)], w=[puk])
                        STT("dve", CT[:, h, kc, :], CT[:, h, kc, :], decay, pu[:], ALU.mult, ALU.add,
                            r=[puk, "Abc", ("CT", h)], w=[("CT", h)])
                    ACT(sm[:, hb:hb + 1], p_den[:, 0:1], AF.Abs, r=["pS"], w=[("sm_r", hb)])
                    TS("dve", sm[:, hb:hb + 1], sm[:, hb:hb + 1], 1.0, None, ALU.max, r=[("sm_r", hb)], w=[("sm_r", hb)])
                    sc.add("dve", lambda e, hb=hb: e.reciprocal(out=sm[:, hb:hb + 1], in_=sm[:, hb:hb + 1]),
                           reads=[("sm_r", hb)], writes=[("sm_r", hb)])
                    STT("dve", nst[:, h, :], nst[:, h, :], decay, p_nup.rearrange("p (k t) -> p k t", t=2)[:, :, 0], ALU.mult, ALU.add,
                        r=["pS", "Abc", "nst"], w=["nst"])
                    CP("pool", nbb[:, h, :, :], nst[:, h, :].unsqueeze(2).to_broadcast([128, 4, 2]), r=["nst"], w=["nbb"])

                def epiA(h):
                    hb = h % 2
                    ACT(sigo[:], pO[:], AF.Sigmoid, r=["pO"], w=["sigo"])
                    PRELOAD(AF.Sqrt)
                    STT("dve", hg[:], pN[:], sm[:, hb:hb + 1], sigo[:], ALU.mult, ALU.mult,
                        r=["pN", ("sm_r", hb), "sigo"], w=["hg"])
                    sc.add("dve", lambda e: e.bn_stats(out=bnst[:], in_=hg[:]), reads=["hg"], writes=["bnst"])
                    sc.add("dve", lambda e: e.bn_aggr(out=sm[:, 4:6], in_=bnst[:]), reads=["bnst"], writes=["sm_mv"])
                    ACT(sm[:, 6:7], sm[:, 5:6], AF.Sqrt, bias=epsb[:, 0:1], r=["sm_mv", "epsb"], w=["sm_rstd"])
                    sc.add("dve", lambda e: e.reciprocal(out=sm[:, 6:7], in_=sm[:, 6:7]), r=["sm_rstd"], w=["sm_rstd"])
                    STT("dve", sm[:, 7:8], sm[:, 4:5], -1.0, sm[:, 6:7], ALU.mult, ALU.mult,
                        r=["sm_mv", "sm_rstd"], w=["sm_nmr"])
                    ACT(hn[:], hg[:], AF.Identity, bias=sm[:, 7:8], scale=sm[:, 6:7],
                        r=["hg", "sm_rstd", "sm_nmr"], w=["hn"])
                    PRELOAD(AF.Silu)

                def epiB(h):
                    for vc in range(4):
                        TR(p_hT[:, vc * 128:(vc + 1) * 128], hn[:, vc * 128:(vc + 1) * 128], identb[:],
                           r=["hn", "identb"], w=["pS"])
                    pb, pk = (pA, "pA") if h % 2 == 0 else (pB, "pB")
                    ACT(sigo[:], pb[:], AF.Silu, r=[pk], w=["sigo"])
                    PRELOAD(AF.Exp)
                    for vc in range(4):
                        oc = 4 * h + vc
                        STT("dve", tmpy[:, vc, :], p_hT[:, vc * 128:(vc + 1) * 128], gng[:, oc:oc + 1], xc32[:, oc, :],
                            ALU.mult, ALU.add, r=["pS", "avec", ("xc32", oc)], w=["hg"])
                    TT("pool", yT[:, 4 * h:4 * h + 4, :], tmpy, sz, ALU.mult, r=["hg", "sigo"], w=[("yT", h)])

                proj(0)
                prep(0)
                for h in range(HA):
                    scan(h)
                    if h + 1 < HA:
                        prep(h + 1)
                    epiA(h)
                    if h + 1 < HA:
                        proj(h + 1)
                    epiB(h)
                if DBG == 7:
                    continue
                ally = [("yT", h) for h in range(HA)]
                for j in range(4):
                    W, wk_ = next_block()
                    Wv = W[:].rearrange("p (k c) -> p k c", k=16)
                    pb, pk = (pA, "pA") if j < 2 else (pB, "pB")
                    for i in range(2):
                        dc = 2 * j + i
                        col = (dc % 4) * 128
                        for cc in range(NIC):
                            MM(pb[:, col:col + 128], Wv[:, cc, i * 128:(i + 1) * 128], yT[:, cc, :],
                               start=(cc == 0), stop=(cc == NIC - 1), r=[wk_] + ally, w=[pk])
                    if j % 2 == 1:
                        d0 = (j // 2) * 4
                        STT("dve", vbuf[:, d0:d0 + 4, :], xT[:, d0:d0 + 4, tok], ALPHA,
                            pb[:].rearrange("p (c t) -> p c t", c=4), ALU.mult, ALU.add, r=[pk, xk], w=["vbuf"])
                ln_tail(t, l, last_layer)

        if n_layers > 2:
            sc.barrier(lambda e: e.memset(sm[:, 0:1], 0.0))
            GROUPS = ((128, 1), (512, 4), (2048, 16))
            slopes = [2.0 ** (-8.0 * (i + 1.0) / 8.0) for i in range(8)]
            OD = CT[:].rearrange("p h k v -> p (h k v)").rearrange("p (a t) -> p a t", a=2)
            QTs = [xm32[:].rearrange("p c t -> p (c t)")[:, 0:2048].bitcast(BF16),
                   xmcb[:].rearrange("p c t -> p (c t)")]
            KTs = [xc32[:].rearrange("p c t -> p (c t)").bitcast(BF16), qkT[:].rearrange("p c t -> p (c t)")]
            Vts = [bd[:].rearrange("p a c m -> p (a c m)")[:, 0:4096].rearrange("p (b d) -> p b d", d=128),
                   ktvt[:].rearrange("p (b d) -> p b d", d=128)]
            sqf = sq[:].rearrange("p c t -> p (c t)")
            tabs = [hg[:, 0:256].rearrange("p (a q) -> p a q", a=2), sqf[:, 0:256].rearrange("p (a q) -> p a q", a=2)]
            s_bufs = [sigo[:, 0:256].rearrange("p (a q) -> p a q", a=2),
                      sigo[:, 256:512].rearrange("p (a q) -> p a q", a=2),
                      hg[:, 256:512].rearrange("p (a q) -> p a q", a=2)]
            e_bufs = [hn[:, 0:256].rearrange("p (a q) -> p a q", a=2),
                      hn[:, 256:512].rearrange("p (a q) -> p a q", a=2),
                      kw[0][:, 0:256].rearrange("p (a q) -> p a q", a=2)]
            ctf = CTb0[:].rearrange("p k v -> p (k v)").bitcast(F32)
            JMf = ctf[:, 0:512]
            JM = JMf.rearrange("p (a q) -> p a q", a=4)
            recb = ctf[:, 512:1024]
            szb = Abc[:]
            ones128 = qs[0][:, 0, :]
            st0 = kw[1][:]
            st1 = qs[1][:].rearrange("p c t -> p (c t)")
            DMA("sp", JMf, jm_d[:, :], "const2", w=["JM"])
            sc.add("pool", lambda e: e.memset(ones128, 1.0), w=["ones128"])
            nlb = n_layers - 2
            wkvv = b_wkv_b.rearrange("(kc p) c -> p kc c", p=128)
            evi = 0
            kvst = [(st0, "st0"), (st1, "st1"), (hn[:], "hn"), (sigo[:].bitcast(BF16)[:, 0:512], "sigo")]
            kvps = [(pA, "pA"), (pB, "pB"), (pO, "pO"), (pN, "pN")]
            for tt in range(S // 512):
                t5 = slice(tt * 512, (tt + 1) * 512)
                xks = [("xT", 4 * tt + i) for i in range(4)]
                for j in range(12):
                    slot = j % 3
                    Wv = ring[slot][:].rearrange("p (k c) -> p k c", k=8)
                    DMA("sp", Wv, wkvv[:, :, j * 512:(j + 1) * 512], ("ring", slot), r=[("wkvb", j)], w=[("ring", slot)])
                    g, isv, half = j // 4, (j // 2) % 2, j % 2
                    for q4 in range(4):
                        pb, pk = kvps[evi % 4]
                        stg, sk = kvst[evi % 4]
                        if not isv:
                            for kc in range(8):
                                MM(pb[:], Wv[:, kc, q4 * 128:(q4 + 1) * 128], xT[:, kc, t5], start=(kc == 0),
                                   stop=(kc == 7), r=[("ring", slot)] + xks, w=[pk])
                            CP("act" if evi % 2 == 0 else "dve", stg, pb[:], r=[pk], w=[sk])
                            DMA("sp", kt_d[g * 8 + half * 4 + q4, :, t5], stg, ("kvst", evi % 4), r=[sk], w=["kt_d"])
                        else:
                            tk = slice(tt * 512 + q4 * 128, tt * 512 + (q4 + 1) * 128)
                            for kc in range(8):
                                MM(pb[:], xT[:, kc, tk], Wv[:, kc, :], start=(kc == 0), stop=(kc == 7),
                                   r=[("ring", slot)] + xks, w=[pk])
                            CP("act" if evi % 2 == 0 else "dve", stg, pb[:], r=[pk], w=[sk])
                            DMA("sp", v_d[g, tk, half * 512:(half + 1) * 512], stg, ("kvst", evi % 4), r=[sk], w=["v_d"])
                        evi += 1
            allx = [("xT", i) for i in range(NT)]
            for lb in range(nlb):
                l = 2 + lb
                last_layer = (l == n_layers - 1)
                winv = b_win_b[lb].rearrange("(kc p) c -> p kc c", p=128)
                woutv = b_wout_b[lb].rearrange("(kc p) c -> p kc c", p=128)
                seq = [(h, g) for h in range(8) for g in range(3)]

                def load(n, lb=lb, winv=winv):
                    h, g = seq[n]
                    dil = GROUPS[g][1]
                    NB = 32 // dil
                    p = n % 2
                    rs = 0 if p == 0 else 2
                    Wq = ring[rs][:, 0:1024].rearrange("p (k c) -> p k c", k=8)
                    DMA("sp", Wq, winv[:, :, g * 1024 + h * 128:g * 1024 + (h + 1) * 128], ("ring", rs),
                        r=[("bwinb", lb, (g * 1024 + h * 128) // 512)], w=[("ring", rs)])
                    DMA("sp", KTs[p], kt_d[g * 8 + h, :, :], ("ktl", p), r=["kt_d"], w=[("KT", p)])
                    Vt = Vts[p]
                    if dil == 1:
                        for b0 in range(0, 32, 8):
                            vsrc = bass.AP(v_d.tensor, g * S * D + h * 128 + b0 * 128 * D,
                                           [[D, 128], [128 * D, 8], [1, 128]])
                            DMA("sp", Vt[:, b0:b0 + 8, :], vsrc, ("vtl", p), r=["v_d"], w=[("Vt", p)])
                    else:
                        for r0 in range(dil):
                            vsrc = bass.AP(v_d.tensor, g * S * D + h * 128 + r0 * D,
                                           [[dil * D, 128], [128 * dil * D, NB], [1, 128]])
                            DMA("sp", Vt[:, r0 * NB:(r0 + 1) * NB, :], vsrc, ("vtl", p), r=["v_d"], w=[("Vt", p)])
                    STT("dve", tabs[p], JM[:, 0:2, :], -slopes[h] * dil, JM[:, 2:4, :], ALU.mult, ALU.add,
                        r=["JM"], w=[("tab", p), ("sq" if p == 1 else "hg")])
                    if g == 1:
                        Wz = ring[1][:, 0:1024].rearrange("p (k c) -> p k c", k=8)
                        DMA("sp", Wz, winv[:, :, 3 * 1024 + h * 128:3 * 1024 + (h + 1) * 128], ("ring", 1),
                            r=[("bwinb", lb, (3 * 1024 + h * 128) // 512)], w=[("ring", 1)])

                def qproj_tile(n, tt):
                    p = n % 2
                    rs = 0 if p == 0 else 2
                    Wq = ring[rs][:, 0:1024].rearrange("p (k c) -> p k c", k=8)
                    pb, pk = (pA, "pA") if tt % 2 == 0 else (pB, "pB")
                    for kc in range(8):
                        MM(pb[:], Wq[:, kc, :], xT[:, kc, tt * 512:(tt + 1) * 512], start=(kc == 0),
                           stop=(kc == 7), r=[("ring", rs)] + allx, w=[pk])
                    ACT(QTs[p][:, tt * 512:(tt + 1) * 512], pb[:], AF.Copy, scale=128.0 ** -0.5, r=[pk], w=[("QT", p)])

                ps1s = ((pO, "pO"), (pN, "pN"), (pU[0], "pU0"))
                ps2s = ((pBu, "pBu"), (pBm, "pBm"), (pS, "pS"))

                def stage1(n, ui):
                    h, g = seq[n]
                    dil = GROUPS[g][1]
                    NB = 32 // dil
                    p = n % 2
                    QT, KT, tab = QTs[p], KTs[p], tabs[p]
                    r_, nb = ui // NB, ui % NB
                    c0 = r_ + dil * 128 * nb
                    qc = QT[:, c0:c0 + dil * 127 + 1:dil]
                    na = 2 if nb > 0 else 1
                    s_sb, e_sb = s_bufs[ui % 3], e_bufs[ui % 3]
                    sk_, ek_ = ("s_sb", ui % 3), ("e_sb", ui % 3)
                    ps1, pk1 = ps1s[ui % 3]
                    MM(ps1[:, 0:128], KT[:, c0:c0 + dil * 127 + 1:dil], qc, r=[("KT", p), ("QT", p)], w=[pk1])
                    if nb > 0:
                        cp_ = c0 - dil * 128
                        MM(ps1[:, 128:256], KT[:, cp_:cp_ + dil * 127 + 1:dil], qc, r=[("KT", p), ("QT", p)], w=[pk1])
                    TT("dve", s_sb[:, 0:na, :], ps1[:, 0:na * 128].rearrange("p (a q) -> p a q", a=na),
                       tab[:, 0:na, :], ALU.add, r=[pk1, ("tab", p)], w=[sk_])
                    ACT(e_sb[:, 0:na, :], s_sb[:, 0:na, :], AF.Exp, r=[sk_], w=[ek_])

                def stage2(n, ui):
                    h, g = seq[n]
                    dil = GROUPS[g][1]
                    NB = 32 // dil
                    p = n % 2
                    Vt = Vts[p]
                    r_, nb = ui // NB, ui % NB
                    c0 = r_ + dil * 128 * nb
                    e_sb = e_bufs[ui % 3]
                    ek_ = ("e_sb", ui % 3)
                    ps2, pk2 = ps2s[ui % 3]
                    blk = r_ * NB + nb
                    MM(ps2[:, 0:128], Vt[:, blk, :], e_sb[:, 0, :], start=True, stop=(nb == 0),
                       r=[("Vt", p), ek_], w=[pk2])
                    if nb > 0:
                        MM(ps2[:, 0:128], Vt[:, blk - 1, :], e_sb[:, 1, :], start=False, stop=True,
                           r=[("Vt", p), ek_], w=[pk2])
                    MM(ps2[:, 128:256], ones128, e_sb[:, 0, :], start=True, stop=(nb == 0),
                       r=["ones128", ek_], w=[pk2])
                    if nb > 0:
                        MM(ps2[:, 128:256], ones128, e_sb[:, 1, :], start=False, stop=True,
                           r=["ones128", ek_], w=[pk2])
                    odv = OD[:, :, c0:c0 + dil * 127 + 1:dil]
                    pin = ps2[:, 0:256].rearrange("p (a q) -> p a q", a=2)
                    if g == 0:
                        odk = [("OD", nb // 4)]
                    elif g == 1:
                        odk = [("OD", nb)]
                    else:
                        odk = [("OD", 4 * nb + i_) for i_ in range(4)]
                    if g == 0:
                        CP("act", odv, pin, r=[pk2], w=odk)
                    else:
                        TT("dve", odv, odv, pin, ALU.add, r=[pk2] + odk, w=odk)

                def gating(h):
                    Wz = ring[1][:, 0:1024].rearrange("p (k c) -> p k c", k=8)
                    for tt in range(8):
                        t5 = slice(tt * 512, (tt + 1) * 512)
                        pb, pk = (pA, "pA") if tt % 2 == 0 else (pB, "pB")
                        for kc in range(8):
                            MM(pb[:], Wz[:, kc, :], xT[:, kc, t5], start=(kc == 0), stop=(kc == 7),
                               r=[("ring", 1)] + allx, w=[pk])
                        vfl = vbuf[:].rearrange("p c t -> p (c t)")
                        rb_, rk_ = (recb, "recb") if tt % 2 == 0 else (vfl[:, 0:512], "recb2")
                        sz_, zk_ = (szb, "szb") if tt % 2 == 0 else (vfl[:, 512:1024], "szb2")
                        ACT(sz_, pb[:], AF.Silu, r=[pk], w=[zk_])
                        sc.add("dve", lambda e, t5=t5, rb_=rb_: e.reciprocal(out=rb_, in_=OD[:, 1, t5]),
                               r=[("OD", tt)], w=[rk_])
                        TT("pool", rb_, rb_, OD[:, 0, t5], ALU.mult, r=[rk_, ("OD", tt)], w=[rk_])
                        stg, sk = (st0, "st0") if tt % 2 == 0 else (st1, "st1")
                        TT("pool", stg, rb_, sz_, ALU.mult, r=[rk_, zk_], w=[sk])
                        DMA("sp", y_d[h, :, t5], stg, ("yst", tt % 2), r=[sk], w=["y_d"])

                load(0)
                for tt in range(8):
                    qproj_tile(0, tt)
                for n in range(len(seq)):
                    h, g = seq[n]
                    if n + 1 < len(seq):
                        load(n + 1)
                    for i in range(32 + 2):
                        if i < 32:
                            stage1(n, i)
                        if i >= 2:
                            stage2(n, i - 2)
                        if n + 1 < len(seq) and i % 4 == 3 and i < 32:
                            qproj_tile(n + 1, i // 4)
                    if g == 2:
                        gating(h)
                Wo0 = ring[0][:].rearrange("p (k c) -> p k c", k=8)
                Wo1 = ring[1][:].rearrange("p (k c) -> p k c", k=8)
                DMA("sp", Wo0, woutv[:, :, 0:512], ("ring", 0), r=[("bwoutb", lb, 0)], w=[("ring", 0)])
                DMA("sp", Wo1, woutv[:, :, 512:1024], ("ring", 1), r=[("bwoutb", lb, 1)], w=[("ring", 1)])
                xmf = xm32[:].rearrange("p c t -> p (c t)")
                vbs = [vbuf, xmf[:, 0:1024].rearrange("p (c t) -> p c t", c=NCH)]
                sqs = [sq, xmf[:, 1024:2048].rearrange("p (c t) -> p c t", c=NCH)]

                def p7(t, lb=lb):
                    par = t % 2
                    tok = slice(t * T, (t + 1) * T)
                    xk = ("xT", t)
                    yk = ("yTf", par)
                    yv = yT[:, 8 * par:8 * par + 8, :]
                    DMA("sp", yv, y_d[:, :, tok].rearrange("h p t -> p h t"), ("ytl", par), r=["y_d"], w=[yk])
                    for dc in range(NCH):
                        pb, pk = (pA, "pA") if dc < 4 else (pB, "pB")
                        Wo, wk_ = (Wo0, ("ring", 0)) if dc < 4 else (Wo1, ("ring", 1))
                        col = (dc % 4) * 128
                        for hh in range(8):
                            MM(pb[:, col:col + 128], Wo[:, hh, col:col + 128], yv[:, hh, :], start=(hh == 0),
                               stop=(hh == 7), r=[wk_, yk], w=[pk])
                        if dc % 4 == 3:
                            d0 = (dc // 4) * 4
                            STT("dve", vbs[par][:, d0:d0 + 4, :], xT[:, d0:d0 + 4, tok], ALPHA,
                                pb[:].rearrange("p (c t) -> p c t", c=4), ALU.mult, ALU.add, r=[pk, xk],
                                w=[("vbf", par)])

                p7(0)
                for t in range(NT):
                    if t + 1 < NT:
                        p7(t + 1)
                    ln_tail(t, l, last_layer, vb=vbs[t % 2], sqb=sqs[t % 2], vk=("vbf", t % 2), sk=("sqf", t % 2))
        counts = sc.emit(nc, st, final_wait_ops=final_ops[-1:] if final_ops else [])
        print("op counts", counts, flush=True)
    return nc


final_ops = []
import os
DBG = int(os.environ.get("KDBG", "0"))


def _prep(inputs):
    f = np.float32
    ident = np.eye(128, dtype=f)
    utri = np.triu(np.ones((128, 128), f))
    mneg = np.where(utri > 0, 0.0, -30000.0).astype(f)
    ones = np.ones((128, 128), f)
    cst = np.stack([ident, utri, mneg, ones, ones, ones], axis=1)
    shared = {"cst": np.ascontiguousarray(cst)}
    for l in range(2):
        shared["a_win%d" % l] = np.ascontiguousarray(inputs["a_w_in"][l], dtype=f)
        shared["a_wout%d" % l] = np.ascontiguousarray(inputs["a_w_out"][l], dtype=f)
        bds = []
        for nm in ("a_wq", "a_wk", "a_wv"):
            w = np.asarray(inputs[nm][l], dtype=f)
            full = np.zeros((NIC, 128, 128), f)
            wr = w.reshape(NIC, 32, 4, 4)
            for n in range(32):
                full[:, n * 4:(n + 1) * 4, n * 4:(n + 1) * 4] = np.transpose(wr[:, n], (0, 2, 1))
            bds.append(np.transpose(full, (1, 0, 2)))
        shared["a_bd%d" % l] = np.ascontiguousarray(np.stack(bds, axis=1).reshape(128, -1))
        wif = np.asarray(inputs["a_w_if"][l], dtype=f).reshape(48, 128, 8).transpose(1, 0, 2)
        shared["a_wif%d" % l] = np.ascontiguousarray(wif.reshape(128, -1))
        bif = np.broadcast_to(np.asarray(inputs["a_b_if"][l], dtype=f)[None, :], (128, 8))
        cwv = np.asarray(inputs["a_conv_w"][l], dtype=f).reshape(4, NIC, 128).transpose(2, 1, 0).reshape(128, 64)
        cbv = np.asarray(inputs["a_conv_b"][l], dtype=f).reshape(NIC, 128).T
        gng = np.asarray(inputs["a_gn_g"][l], dtype=f).reshape(NIC, 128).T
        skp = np.asarray(inputs["a_skip"][l], dtype=f).reshape(NIC, 128).T
        shared["a_vec%d" % l] = np.ascontiguousarray(np.concatenate([bif, cwv, cbv, gng, skp], axis=1))
    kk = np.arange(128)[:, None]
    qq = np.arange(128)[None, :]
    jcur = np.where(qq >= kk, qq - kk, 0).astype(f)
    mcur = np.where(qq >= kk, 0.0, -30000.0).astype(f)
    jprev = np.where(kk >= qq, qq + 128 - kk, 0).astype(f)
    mprev = np.where(kk >= qq, 0.0, -30000.0).astype(f)
    shared["jm"] = np.ascontiguousarray(np.stack([jcur, jprev, mcur, mprev], axis=1).reshape(128, 512))
    shared["b_wkv"] = np.ascontiguousarray(inputs["b_w_kv"], dtype=f)
    for l in range(2):
        shared["b_win%d" % l] = np.ascontiguousarray(inputs["b_w_in"][l], dtype=f)
        shared["b_wout%d" % l] = np.ascontiguousarray(inputs["b_w_out"][l], dtype=f)
    lng = np.asarray(inputs["ln_g"], dtype=f).reshape(DEPTH, NCH, 128)
    lnb = np.asarray(inputs["ln_b"], dtype=f).reshape(DEPTH, NCH, 128)
    lnv = np.stack([lng, lnb], axis=1)
    shared["lnv"] = np.ascontiguousarray(lnv.transpose(3, 0, 1, 2).reshape(128, -1))
    return shared


def kernel(_n_layers=4, _n_tiles=NT, **inputs):
    x = np.ascontiguousarray(inputs["x"], dtype=np.float32)
    del final_ops[:]
    nc = build(_n_layers, _n_tiles)
    shared = _prep(inputs)
    in_maps = []
    for c in range(8):
        m = dict(shared)
        m["x"] = x[c] if c < 4 else np.zeros_like(x[0])
        in_maps.append(m)
    res = run_bass_kernel_spmd(nc, in_maps, core_ids=list(range(8)))
    return np.stack([res.results[c]["out"] for c in range(4)], axis=0)
```

```python
import concourse.bass as bass
import concourse.mybir as mybir


PSUM_KEYS = {"pA", "pB", "pO", "pN", "pU0", "pS", "pBu", "pBm"}


class Op:
    __slots__ = ("eng", "fn", "deps", "signal", "cnt", "is_dma", "dsem", "idx")

    def __init__(self, eng, fn, is_dma=False, dsem=None):
        self.eng = eng
        self.fn = fn
        self.deps = set()
        self.signal = False
        self.cnt = None
        self.is_dma = is_dma
        self.dsem = dsem
        self.idx = None


class Sched:
    ENGS = ("pe", "act", "dve", "pool", "sp")

    def __init__(self):
        self.ops = []
        self.last_w = {}
        self.readers = {}

    def barrier(self, fn):
        keys = list(self.last_w.keys()) + list(self.readers.keys())
        return self.add("dve", fn, reads=(), writes=list(dict.fromkeys(keys + ["__epoch__"])))

    def add(self, eng, fn, reads=(), writes=(), dma=None, r=None, w=None):
        if r is not None:
            reads = r
        if w is not None:
            writes = w
        pr = [k for k in reads if k in PSUM_KEYS]
        if pr:
            writes = list(writes) + [k for k in pr if k not in writes]
            reads = [k for k in reads if k not in PSUM_KEYS]
        reads = list(reads) + ["__epoch__"]
        op = Op(eng, fn, is_dma=dma is not None, dsem=dma)
        op.idx = len(self.ops)
        deps = set()
        for r in reads:
            w = self.last_w.get(r)
            if w is not None:
                deps.add(w)
        for w_ in writes:
            w = self.last_w.get(w_)
            if w is not None:
                deps.add(w)
            for rd in self.readers.get(w_, ()):
                deps.add(rd)
        deps.discard(op.idx)
        fdeps = set()
        for d in deps:
            dop = self.ops[d]
            if dop.is_dma:
                fdeps.add(d)
                continue
            if op.is_dma:
                fdeps.add(d)
                continue
            if dop.eng == eng:
                if eng == "pe":
                    continue
                israw = False
                for r in reads:
                    if self.last_w.get(r) == d:
                        israw = True
                        break
                if not israw:
                    continue
            fdeps.add(d)
        op.deps = fdeps
        self.ops.append(op)
        for r in reads:
            self.readers.setdefault(r, []).append(op.idx)
        for w_ in writes:
            self.last_w[w_] = op.idx
            self.readers[w_] = []
        return op

    def emit(self, nc, stack, final_wait_ops=()):
        ops = self.ops
        for op in ops:
            for d in op.deps:
                ops[d].signal = True
        for op in final_wait_ops:
            op.signal = True
        esem = {e: stack.enter_context(nc.semaphore("s_" + e)) for e in self.ENGS}
        dsems = {}
        for op in ops:
            if op.is_dma and op.dsem not in dsems:
                dsems[op.dsem] = stack.enter_context(nc.semaphore("d_%s" % (op.dsem,)))
        ecnt = {e: 0 for e in self.ENGS}
        dcnt = {k: 0 for k in dsems}
        dma_cum_at = []
        dlist = {k: [] for k in dsems}
        for op in ops:
            if op.is_dma:
                dcnt[op.dsem] += 16
                op.cnt = dcnt[op.dsem]
                dlist[op.dsem].append(op.idx)
            elif op.signal:
                ecnt[op.eng] += 1
                op.cnt = ecnt[op.eng]
        import bisect
        per_eng = {e: [] for e in self.ENGS}
        seen = {e: {} for e in self.ENGS}
        for op in ops:
            waits = {}
            for d in op.deps:
                dop = ops[d]
                if dop.is_dma:
                    lst = dlist[dop.dsem]
                    j = bisect.bisect_left(lst, op.idx) - 1
                    val = ops[lst[j]].cnt
                    key = ("d", dop.dsem)
                else:
                    val = dop.cnt
                    key = ("e", dop.eng)
                if val > waits.get(key, 0):
                    waits[key] = val
            wl = []
            sn = seen[op.eng]
            for key, val in waits.items():
                if sn.get(key, 0) >= val:
                    continue
                sn[key] = val
                sem = dsems[key[1]] if key[0] == "d" else esem[key[1]]
                wl.append((sem, val))
            per_eng[op.eng].append((op, wl))
        self.esem, self.dsems = esem, dsems
        finals = []
        for op in final_wait_ops:
            if op.is_dma:
                finals.append((dsems[op.dsem], dcnt[op.dsem]))
            else:
                finals.append((esem[op.eng], op.cnt))

        def run(engname, eng):
            for op, wl in per_eng[engname]:
                for sem, val in wl:
                    eng.wait_ge(sem, val)
                ins = op.fn(eng)
                if op.is_dma:
                    ins.then_inc(dsems[op.dsem], 16)
                elif op.signal:
                    ins.then_inc(esem[op.eng], 1)

        with nc.Block() as block:
            @block.tensor
            def _(e):
                run("pe", e)

            @block.scalar
            def _(e):
                run("act", e)

            @block.vector
            def _(e):
                run("dve", e)

            @block.gpsimd
            def _(e):
                run("pool", e)

            @block.sync
            def _(e):
                run("sp", e)
                for sem, val in finals:
                    e.wait_ge(sem, val)
        return {e: len(per_eng[e]) for e in self.ENGS}


import math
import numpy as np
from contextlib import ExitStack
from concourse.bass_utils import run_bass_kernel_spmd


F32 = mybir.dt.float32
BF16 = mybir.dt.bfloat16
AF = mybir.ActivationFunctionType
ALU = mybir.AluOpType

S = 4096
D = 1024
NCH = D // 128
INNER = 2048
NIC = INNER // 128
HA = 4
DH = 512
T = 128
NT = S // T
DEPTH = 4
ALPHA = (2 * DEPTH) ** 0.25
LN_EPS = 1e-5
LNK = math.log(DH ** -0.5)


def build(n_layers=4, n_tiles=NT):
    nc = bass.Bass("TRN2", target_bir_lowering=False)

    def din(name, shape, dt=F32):
        return nc.dram_tensor(name, list(shape), dt, kind="ExternalInput").ap()

    x_d = din("x", [S, D])
    out_d = nc.dram_tensor("out", [S, D], F32, kind="ExternalOutput").ap()
    cst_d = din("cst", [128, 6, 128])
    a_win_d = [din("a_win%d" % l, [D, 3 * INNER]) for l in range(2)]
    a_wout_d = [din("a_wout%d" % l, [INNER, D]) for l in range(2)]
    a_bd_d = [din("a_bd%d" % l, [128, 3 * NIC * 128]) for l in range(2)]
    a_wif_d = [din("a_wif%d" % l, [128, 48 * 8]) for l in range(2)]
    a_vec_d = [din("a_vec%d" % l, [128, 8 + 64 + 16 * 3]) for l in range(2)]
    ln_d = din("lnv", [128, DEPTH * 2 * NCH])
    a_win_b = [nc.dram_tensor("a_winb%d" % l, [D, 3 * INNER], BF16, kind="Internal").ap() for l in range(2)]
    a_wout_b = [nc.dram_tensor("a_woutb%d" % l, [INNER, D], BF16, kind="Internal").ap() for l in range(2)]

    jm_d = din("jm", [128, 512])
    b_wkv_d = din("b_wkv", [D, 6 * D])
    b_win_d = [din("b_win%d" % l, [D, 4 * D]) for l in range(2)]
    b_wout_d = [din("b_wout%d" % l, [D, D]) for l in range(2)]
    b_wkv_b = nc.dram_tensor("b_wkvb", [D, 6 * D], BF16, kind="Internal").ap()
    b_win_b = [nc.dram_tensor("b_winb%d" % l, [D, 4 * D], BF16, kind="Internal").ap() for l in range(2)]
    b_wout_b = [nc.dram_tensor("b_woutb%d" % l, [D, D], BF16, kind="Internal").ap() for l in range(2)]
    kt_d = nc.dram_tensor("kt_d", [24, 128, S], BF16, kind="Internal").ap()
    v_d = nc.dram_tensor("v_d", [3, S, D], BF16, kind="Internal").ap()
    y_d = nc.dram_tensor("y_d", [8, 128, S], BF16, kind="Internal").ap()

    sc = Sched()
    with ExitStack() as st:
        def sb(name, shape, dt):
            return st.enter_context(nc.sbuf_tensor(name, list(shape), dt))

        def ps(name, shape, dt):
            return st.enter_context(nc.psum_tensor(name, list(shape), dt))

        def MM(out, lhsT, rhs, start=True, stop=True, r=(), w=()):
            return sc.add("pe", lambda e: e.matmul(out, lhsT=lhsT, rhs=rhs, start=start, stop=stop), r, w)

        def TR(out, in_, ident, r=(), w=()):
            return sc.add("pe", lambda e: e.transpose(out=out, in_=in_, identity=ident), r, w)

        def ACT(out, in_, func, bias=None, scale=None, r=(), w=()):
            kw = {}
            if bias is not None:
                kw["bias"] = bias
            if scale is not None:
                kw["scale"] = scale
            return sc.add("act", lambda e: e.activation(out=out, in_=in_, func=func, **kw), r, w)

        def CP(eng, out, in_, r=(), w=()):
            if eng == "act":
                return sc.add("act", lambda e: e.copy(out=out, in_=in_), r, w)
            return sc.add(eng, lambda e: e.tensor_copy(out=out, in_=in_), r, w)

        def TT(eng, out, in0, in1, op, r=(), w=()):
            return sc.add(eng, lambda e: e.tensor_tensor(out=out, in0=in0, in1=in1, op=op), r, w)

        def TS(eng, out, in0, s1, s2, op0, op1=None, r=(), w=()):
            if op1 is None:
                return sc.add(eng, lambda e: e.tensor_scalar(out=out, in0=in0, scalar1=s1, scalar2=None, op0=op0), r, w)
            return sc.add(eng, lambda e: e.tensor_scalar(out=out, in0=in0, scalar1=s1, scalar2=s2, op0=op0, op1=op1), r, w)

        def STT(eng, out, in0, scalar, in1, op0, op1, r=(), w=()):
            return sc.add(eng, lambda e: e.scalar_tensor_tensor(out=out, in0=in0, scalar=scalar, in1=in1, op0=op0, op1=op1), r, w)

        pl_cnt = [0]

        def PRELOAD(func):
            pl_cnt[0] += 1
            return sc.add("act", lambda e: e.activation(out=sm[:, 11:12], in_=sm[:, 10:11], func=func),
                          ["sm_dummy"], [("sm_dummy_out", pl_cnt[0])])

        def DMA(q, out, in_, sem, r=(), w=()):
            return sc.add(q, lambda e: e.dma_start(out=out, in_=in_), r, w, dma=sem)

        xT = sb("xT", [128, NCH, S], BF16)
        cst = sb("cst_sb", [128, 6, 128], F32)
        identf, utri, mneg, onesf = cst[:, 0, :], cst[:, 1, :], cst[:, 2, :], cst[:, 3, :]
        identb = sb("identb", [128, 128], BF16)
        onesb = sb("onesb", [128, 2], BF16)
        epsb = sb("epsb", [128, 1], F32)
        mneg4 = sb("mneg4", [128, 4, 128], F32)
        lnv = sb("lnv_sb", [128, DEPTH, 2, NCH], F32)

        pA = ps("pA", [128, 512], F32)
        pB = ps("pB", [128, 512], F32)
        pO = ps("pO", [128, 512], F32)
        pN = ps("pN", [128, 512], F32)
        pU = [ps("pU0", [128, 512], F32), ps("pU1", [128, 512], F32)]
        pBu = ps("pBu", [128, 512], F32)
        pBm = ps("pBm", [128, 512], F32)

        DMA("sp", cst[:], cst_d[:, :, :], "const", w=["cst"])
        DMA("sp", lnv[:].rearrange("p a b c -> p (a b c)"), ln_d[:, :], "const", w=["lnv"])
        CP("dve", identb[:], identf, r=["cst"], w=["identb"])
        sc.add("dve", lambda e: e.memset(onesb[:], 1.0), w=["onesb"])
        sc.add("dve", lambda e: e.memset(epsb[:], LN_EPS), w=["epsb"])
        for h in range(4):
            CP("pool", mneg4[:, h, :], mneg, r=["cst"], w=["mneg4"])

        nl_a = min(n_layers, 2)
        ring = [sb("ring%d" % i, [128, 4096], BF16) for i in range(3)]
        cv = [0]

        def convert_block(src2d, dst2d, rows, c0, c1, key, sem):
            for r0 in range(0, rows, 1024):
                DMA("pool", dst2d[r0:r0 + 1024, c0:c1], src2d[r0:r0 + 1024, c0:c1], sem, w=[key])

        for l in range(nl_a):
            for i in range(4):
                convert_block(a_win_d[l], a_win_b[l], D, i * 512, (i + 1) * 512, ("winb", l, i), "cv_a%d_m" % l)
            for h_ in range(4):
                for base in (8, 4):
                    i = base + h_
                    convert_block(a_win_d[l], a_win_b[l], D, i * 512, (i + 1) * 512, ("winb", l, i), "cv_a%d_oz" % l)
            for j in range(4):
                convert_block(a_wout_d[l], a_wout_b[l], INNER, j * 256, (j + 1) * 256, ("woutb", l, j), "cv_a%d_w" % l)
        if n_layers > 2:
            for j in range(12):
                convert_block(b_wkv_d, b_wkv_b, D, j * 512, (j + 1) * 512, ("wkvb", j), "cv_kv")
            for lb_ in range(n_layers - 2):
                for j in range(8):
                    convert_block(b_win_d[lb_], b_win_b[lb_], D, j * 512, (j + 1) * 512, ("bwinb", lb_, j), "cv_b%d" % lb_)
                for j in range(2):
                    convert_block(b_wout_d[lb_], b_wout_b[lb_], D, j * 512, (j + 1) * 512, ("bwoutb", lb_, j), "cv_b%d" % lb_)

        vbuf = sb("vbuf", [128, NCH, T], F32)
        sq = sb("sq", [128, NCH, T], F32)
        stage = [vbuf[:].rearrange("p c t -> p (c t)"), sq[:].rearrange("p c t -> p (c t)")]
        for g in range(S // 128):
            b = g % 2
            DMA("sp", stage[b], x_d[g * 128:(g + 1) * 128, :], ("stage", b), w=[("stage", b)])
            for hlf in range(2):
                pb, pk = (pA, "pA") if hlf == 0 else (pB, "pB")
                for j in range(4):
                    c = hlf * 4 + j
                    TR(pb[:, j * 128:(j + 1) * 128], stage[b][:, c * 128:(c + 1) * 128], identf,
                       r=[("stage", b), "cst"], w=[pk])
                CP("act" if hlf == 0 else "dve", xT[:, hlf * 4:(hlf + 1) * 4, g * 128:(g + 1) * 128],
                   pb[:].rearrange("p (c t) -> p c t", c=4), r=[pk], w=[("xT", g)])

        def ln_tail(t, l, last_layer, vb=None, sqb=None, vk="vbuf", sk="sq"):
            vb = vbuf if vb is None else vb
            sqb = sq if sqb is None else sqb
            tok = slice(t * T, (t + 1) * T)
            xk = ("xT", t)
            TT("dve", sqb[:], vb[:], vb[:], ALU.mult, r=[vk], w=[sk])
            for dc in range(NCH):
                MM(pBu[:, 0:T], onesf, vb[:, dc, :], start=(dc == 0), stop=(dc == NCH - 1), r=["cst", vk], w=["pBu"])
            for dc in range(NCH):
                MM(pBm[:, 0:T], onesf, sqb[:, dc, :], start=(dc == 0), stop=(dc == NCH - 1), r=["cst", sk], w=["pBm"])
            ACT(lnt[:, 0, :], pBu[:, 0:T], AF.Copy, scale=1.0 / D, r=["pBu"], w=["ln_mean", "Abc"])
            TT("pool", lnt[:, 1, :], lnt[:, 0, :], lnt[:, 0, :], ALU.mult, r=["ln_mean"], w=["ln_msq", "Abc"])
            STT("dve", lnt[:, 1, :], pBm[:, 0:T], 1.0 / D, lnt[:, 1, :], ALU.mult, ALU.subtract,
                r=["pBm", "ln_msq"], w=["ln_msq", "Abc"])
            ACT(lnt[:, 2, :], lnt[:, 1, :], AF.Sqrt, bias=epsb[:, 0:1], r=["ln_msq", "epsb"], w=["ln_rstd", "Abc"])
            sc.add("dve", lambda e: e.reciprocal(out=lnt[:, 2, :], in_=lnt[:, 2, :]), r=["ln_rstd"], w=["ln_rstd", "Abc"])
            TT("dve", vb[:], vb[:], lnt[:, 0, :].unsqueeze(1).to_broadcast([128, NCH, T]), ALU.subtract,
               r=[vk, "ln_mean", sk], w=[vk])
            TT("dve", vb[:], vb[:], lnt[:, 2, :].unsqueeze(1).to_broadcast([128, NCH, T]), ALU.mult,
               r=[vk, "ln_rstd"], w=[vk])
            for dc in range(NCH):
                if not last_layer:
                    ACT(xT[:, dc, tok], vb[:, dc, :], AF.Identity, bias=lnv[:, l, 1, dc:dc + 1],
                        scale=lnv[:, l, 0, dc:dc + 1], r=[vk, "lnv"], w=[xk])
                else:
                    ACT(sqb[:, dc, :], vb[:, dc, :], AF.Identity, bias=lnv[:, l, 1, dc:dc + 1],
                        scale=lnv[:, l, 0, dc:dc + 1], r=[vk, "lnv"], w=[sk])
            if last_layer:
                for hlf in range(2):
                    pb, pk = (pA, "pA") if hlf == 0 else (pB, "pB")
                    for j in range(4):
                        TR(pb[:, j * 128:(j + 1) * 128], sqb[:, hlf * 4 + j, :], identf, r=[sk, "cst"], w=[pk])
                    CP("act" if hlf == 0 else "dve", ostage[:, hlf * 512:(hlf + 1) * 512], pb[:], r=[pk], w=[("xc32", i_) for i_ in range(8)])
                final_ops.append(DMA("sp", out_d[t * T:(t + 1) * T, :], ostage, "ostage", r=[("xc32", i_) for i_ in range(8)]))

        if nl_a > 0:
            CT = sb("CT", [128, HA, 4, 512], F32)
            CTb0 = sb("CTb0", [128, 4, 512], BF16)
            CTb = [CTb0, CTb0]
            nst = sb("nst", [128, HA, 4], F32)
            nbb = sb("nbb", [128, HA, 4, 2], BF16)
            bd = sb("bd", [128, 3, NIC, 128], BF16)
            wif = sb("wif", [128, 48, 8], BF16)
            avec = sb("avec", [128, 8 + 64 + 48], F32)
            bif = avec[:, 0:8]
            cw = avec[:, 8:72].rearrange("p (c j) -> p c j", j=4)
            cb = avec[:, 72:88]
            gng = avec[:, 88:104]
            skp = avec[:, 104:120]
            xm32 = sb("xm32", [128, NIC, T + 3], F32)
            xmcb = sb("xmcb", [128, 2 * NIC, T], BF16)
            xmb = xmcb[:, 0:NIC, :]
            xcb = xmcb[:, NIC:2 * NIC, :]
            xc32 = sb("xc32", [128, NIC, T], F32)
            qkT = sb("qkT", [128, 2 * NIC, T], BF16)
            qT = qkT[:, 0:NIC, :]
            kT = qkT[:, NIC:2 * NIC, :]
            ktvt = sb("ktvt", [128, 2 * INNER], BF16)
            ktok = ktvt[:, 0:INNER]
            vtok = ktvt[:, INNER:2 * INNER]
            gsb = sb("gsb", [128, 8], F32)
            gtmp = sb("gtmp", [128, 4], F32)
            lf = sb("lf", [128, 4], F32)
            bias_s = sb("bias_s", [128, 4], F32)
            Abc = sb("Abc", [128, 512], F32)
            WT = [sb("WT%d" % i, [128, 128], F32) for i in range(2)]
            scW = [sb("scW%d" % i, [128, 128], BF16) for i in range(2)]
            qs = [sb("qs%d" % i, [128, 4, 128], BF16) for i in range(2)]
            kw = [sb("kw%d" % i, [128, 512], BF16) for i in range(2)]
            sigo = sb("sigo", [128, 512], F32)
            hg = sb("hg", [128, 512], F32)
            rhsB = hg[:].rearrange("p (h l) -> p h l", h=4)
            tmpy = hg[:].rearrange("p (h l) -> p h l", h=4)
            hn = sb("hn", [128, 512], BF16)
            sz = sigo[:].rearrange("p (h l) -> p h l", h=4)
            yT = sb("yT", [128, NIC, T], BF16)
            vT = yT
            lnt = Abc[:].rearrange("p (h l) -> p h l", h=4)
            ptmp = sq[:, 0, :]
            sm = sb("sm", [128, 16], F32)
            bnst = sb("bnst", [128, 6], F32)
            ostage = xc32[:].rearrange("p c t -> p (c t)")[:, 0:D]
            pS = pU[1]
            p_scT = pS[:, 0:128]
            p_den = pS[:, 128:130]
            p_G = pS[:, 136:144]
            p_btok = pS[:, 144:148]
            p_nup = pS[:, 152:160]
            p_hT = pS[:, 256:512].bitcast(BF16)

        for l in range(nl_a):
            last_layer = (l == n_layers - 1)
            DMA("pool", bd[:].rearrange("p a c m -> p (a c m)"), a_bd_d[l][:, :], "lw", w=["bd"])
            DMA("pool", wif[:].rearrange("p c g -> p (c g)"), a_wif_d[l][:, :], "lw", w=["wif"])
            DMA("sp", avec[:], a_vec_d[l][:, :], "lw2", w=["avec"])
            sc.add("dve", lambda e: e.memset(CT[:].rearrange("p h k v -> p (h k v)"), 0.0), w=[("CT", h_) for h_ in range(HA)])
            sc.add("dve", lambda e: e.memset(nst[:].rearrange("p h k -> p (h k)"), 0.0), w=["nst"])
            sc.add("dve", lambda e: e.memset(sm[:, 10:12], 1.0), w=["sm_dummy"])
            sc.add("pool", lambda e: e.memset(nbb[:].rearrange("p h k t -> p (h k t)"), 0.0), w=["nbb"])
            sc.add("pool", lambda e: e.memset(xm32[:, :, 0:3], 0.0), w=[("xm32", i_) for i_ in range(4)])

            winv = a_win_b[l].rearrange("(kc p) c -> p kc c", p=128)
            woutv = a_wout_b[l].rearrange("(cc p) d -> p cc d", p=128)
            blocks = []
            for t in range(n_tiles):
                for i in range(4):
                    blocks.append(("m", i))
                for h in range(HA):
                    blocks.append(("o", h))
                    blocks.append(("z", h))
                for j in range(4):
                    blocks.append(("w", j))
            state = {"issued": 0, "next": 0}

            def issue_block(n, l=l, blocks=blocks, winv=winv, woutv=woutv):
                kind, i = blocks[n]
                slot = n % 3
                if kind == "m":
                    src = winv[:, :, i * 512:(i + 1) * 512]
                    dst = ring[slot][:].rearrange("p (k c) -> p k c", k=8)
                    rk = ("winb", l, i)
                elif kind == "z":
                    src = winv[:, :, INNER + i * 512:INNER + (i + 1) * 512]
                    dst = ring[slot][:].rearrange("p (k c) -> p k c", k=8)
                    rk = ("winb", l, 4 + i)
                elif kind == "o":
                    src = winv[:, :, 2 * INNER + i * 512:2 * INNER + (i + 1) * 512]
                    dst = ring[slot][:].rearrange("p (k c) -> p k c", k=8)
                    rk = ("winb", l, 8 + i)
                else:
                    src = woutv[:, :, i * 256:(i + 1) * 256]
                    dst = ring[slot][:].rearrange("p (k c) -> p k c", k=16)
                    rk = ("woutb", l, i)
                DMA("sp", dst, src, ("ring", slot), r=[rk], w=[("ring", slot)])

            def next_block(state=state, blocks=blocks, issue_block=issue_block):
                n = state["next"]
                while state["issued"] < min(n + 3, len(blocks)):
                    issue_block(state["issued"])
                    state["issued"] += 1
                state["next"] = n + 1
                slot = n % 3
                return ring[slot], ("ring", slot)

            hslot = [0]
            for t in range(n_tiles):
                tok = slice(t * T, (t + 1) * T)
                xk = ("xT", t)
                for blk in range(4):
                    W, wk_ = next_block()
                    Wv = W[:].rearrange("p (k c) -> p k c", k=8)
                    pb, pk = (pA, "pA") if blk % 2 == 0 else (pB, "pB")
                    for j in range(4):
                        for kc in range(8):
                            MM(pb[:, j * 128:(j + 1) * 128], Wv[:, kc, j * 128:(j + 1) * 128], xT[:, kc, tok],
                               start=(kc == 0), stop=(kc == 7), r=[wk_, xk], w=[pk])
                    CP("act", xm32[:, blk * 4:(blk + 1) * 4, 3:3 + T], pb[:].rearrange("p (c t) -> p c t", c=4),
                       r=[pk], w=[("xm32", blk)])
                if DBG == 1:
                    continue
                hgv = hg[:].rearrange("p (h l) -> p h l", h=4)
                for oc in [3, 7, 11, 15] + [o_ for o_ in range(NIC) if o_ % 4 != 3]:
                    if oc % 4 != 3:
                        TS("dve", xc32[:, oc, :], xm32[:, oc, 0:T], cw[:, oc, 0:1], None, ALU.mult,
                           r=[("xm32", oc // 4), "avec"], w=[("xc32", oc)])
                        for j in range(1, 4):
                            STT("dve", xc32[:, oc, :], xm32[:, oc, j:j + T], cw[:, oc, j:j + 1], xc32[:, oc, :],
                                ALU.mult, ALU.add, r=[("xm32", oc // 4), "avec", ("xc32", oc)], w=[("xc32", oc)])
                    else:
                        ACT(xc32[:, oc, :], xm32[:, oc, 0:T], AF.Copy, scale=cw[:, oc, 0:1],
                            r=[("xm32", oc // 4), "avec"], w=[("xc32", oc)])
                        for j in range(1, 4):
                            sl = hslot[0] % 4
                            hslot[0] += 1
                            ACT(hgv[:, sl, :], xm32[:, oc, j:j + T], AF.Copy, scale=cw[:, oc, j:j + 1],
                                r=[("xm32", oc // 4), "avec", "hg"], w=[("hgs", sl)])
                            TT("pool", xc32[:, oc, :], xc32[:, oc, :], hgv[:, sl, :], ALU.add,
                               r=[("hgs", sl), ("xc32", oc), "hg"], w=[("xc32", oc)])
                    ACT(xc32[:, oc, :], xc32[:, oc, :], AF.Silu, bias=cb[:, oc:oc + 1],
                        r=[("xc32", oc), "avec"], w=[("xc32", oc)])
                    if oc % 4 == 2:
                        g_ = oc // 4
                        CP("act", xcb[:, 4 * g_:4 * g_ + 4, :], xc32[:, 4 * g_:4 * g_ + 4, :],
                           r=[("xc32", 4 * g_ + i_) for i_ in range(4)], w=[("xcb", g_)])
                allxc = [("xc32", oc) for oc in range(NIC)]
                allxm = [("xm32", i_) for i_ in range(4)]
                allxcb = [("xcb", i_) for i_ in range(4)]
                CP("act", xmb, xm32[:, :, 3:3 + T], r=allxm, w=["xmb"])
                CP("pool", xm32[:, :, 0:3], xm32[:, :, T:T + 3], r=allxm, w=allxm)
                TT("dve", xc32[:], xc32[:], skp.unsqueeze(2).to_broadcast([128, NIC, T]), ALU.mult,
                   r=allxc + ["avec"] + allxcb, w=allxc)
                if DBG == 2:
                    continue
                ev = 0
                p3banks = ((pA, "pA"), (pB, "pB"), (pO, "pO"), (pN, "pN"))
                for which, srcb, srck, dst, dk in ((0, xcb, "xcb", qT, "qT"), (1, xcb, "xcb", kT, "kT"),
                                                   (2, xmb, "xmb", vT, ("yT", 0))):
                    for g4 in range(4):
                        pb, pk = p3banks[ev % 4]
                        for j in range(4):
                            oc = g4 * 4 + j
                            MM(pb[:, j * 128:(j + 1) * 128], bd[:, which, oc, :], srcb[:, oc, :],
                               r=["bd", (srck, g4) if srck == "xcb" else srck], w=[pk])
                        if which == 2:
                            CP("dve", dst[:, g4 * 4:(g4 + 1) * 4, :],
                               pb[:].rearrange("p (c t) -> p c t", c=4), r=[pk], w=[dk])
                        else:
                            CP("act" if ev % 2 == 0 else "dve", dst[:, g4 * 4:(g4 + 1) * 4, :],
                               pb[:].rearrange("p (c t) -> p c t", c=4), r=[pk], w=[(dk, g4)])
                        ev += 1
                if DBG == 3:
                    continue
                for i, (srcb, srck) in enumerate(((qT, "qT"), (kT, "kT"), (vT, ("yT", 0)))):
                    for oc in range(NIC):
                        cc = i * NIC + oc
                        MM(p_G, srcb[:, oc, :], wif[:, cc, :], start=(cc == 0), stop=(cc == 47),
                           r=[(srck, oc // 4) if i < 2 else srck, "wif"], w=["pS"])
                TT("dve", gsb[:], p_G, bif, ALU.add, r=["pS", "avec"], w=["gsb"])
                ACT(gtmp[:], gsb[:, 4:8], AF.Exp, scale=-1.0, r=["gsb"], w=["gtmp"])
                ACT(gtmp[:], gtmp[:], AF.Ln, bias=1.0, r=["gtmp"], w=["gtmp"])
                TS("dve", lf[:], gtmp[:], -1.0, None, ALU.mult, r=["gtmp"], w=["lf"])
                TT("pool", rhsB, utri.unsqueeze(1).to_broadcast([128, 4, 128]),
                   lf[:].unsqueeze(2).to_broadcast([128, 4, 128]), ALU.mult, r=["cst", "lf"],
                   w=["hg"] + [("hgs", i_) for i_ in range(4)])
                for which, srcb, srck, dst, dk in ((1, xcb, "xcb", ktok, "ktok"), (2, xmb, "xmb", vtok, "vtok")):
                    for g4 in range(4):
                        pb, pk = p3banks[ev % 4]
                        for j in range(4):
                            oc = g4 * 4 + j
                            MM(pb[:, j * 128:(j + 1) * 128], srcb[:, oc, :], bd[:, which, oc, :],
                               r=["bd", (srck, g4) if srck == "xcb" else srck], w=[pk])
                        CP("act" if ev % 2 == 0 else "dve", dst[:, g4 * 512:(g4 + 1) * 512], pb[:], r=[pk], w=[(dk, g4)])
                        ev += 1
                rB = hg[:]
                MM(pBu[:], onesf, rB, r=["cst", "hg"], w=["pBu"])
                MM(pBm[:], onesf, rB, start=True, stop=False, r=["cst", "hg"], w=["pBm"])
                MM(pBm[:], identf, mneg4[:].rearrange("p h l -> p (h l)"), start=False, stop=True,
                   r=["cst", "mneg4"], w=["pBm"])
                MM(p_btok, utri, lf[:], r=["cst", "lf"], w=["pS"])
                STT("dve", bias_s[:], p_btok, -1.0, gsb[:, 0:4], ALU.mult, ALU.add, r=["pS", "gsb"], w=["bias_s"])
                TS("dve", bias_s[:], bias_s[:], LNK, None, ALU.add, r=["bias_s"], w=["bias_s"])
                ACT(Abc[:], pBu[:], AF.Exp, r=["pBu"], w=["Abc"])
                if DBG == 4:
                    continue
                def prep(h):
                    hb = h % 2
                    hs = slice(h * 128, (h + 1) * 128)
                    hv = slice(h * 512, (h + 1) * 512)
                    for dc in range(4):
                        MM(p_scT, kT[:, 4 * h + dc, :], qT[:, 4 * h + dc, :], start=(dc == 0), stop=(dc == 3),
                           r=[("kT", h), ("qT", h)], w=["pS"])
                    ACT(WT[hb][:], pBm[:, hs], AF.Exp, bias=bias_s[:, h:h + 1], r=["pBm", "bias_s"], w=[("WT", hb)])
                    TT("dve", scW[hb][:], p_scT, WT[hb][:], ALU.mult, r=["pS", ("WT", hb)], w=[("scW", hb)])
                    TT("pool", qs[hb][:], qT[:, 4 * h:4 * h + 4, :],
                       Abc[:, hs].unsqueeze(1).to_broadcast([128, 4, 128]), ALU.mult, r=[("qT", h), "Abc"], w=[("qs", hb)])
                    CP("act", CTb[hb][:].rearrange("p k v -> p (k v)"), CT[:, h, :, :].rearrange("p k v -> p (k v)"),
                       r=[("CT", h)], w=[("CTb", 0)])
                    PRELOAD(AF.Sigmoid)
                    TS("dve", kw[hb][:], ktok[:, hv], WT[hb][:, 127:128], None, ALU.mult,
                       r=[("ktok", h), ("WT", hb)], w=[("kw", hb)])

                def proj(h):
                    Wo, wok = next_block()
                    Wov = Wo[:].rearrange("p (k c) -> p k c", k=8)
                    for kc in range(8):
                        MM(pO[:], xT[:, kc, tok], Wov[:, kc, :], start=(kc == 0), stop=(kc == 7), r=[wok, xk], w=["pO"])
                    Wz, wzk = next_block()
                    Wzv = Wz[:].rearrange("p (k c) -> p k c", k=8)
                    pb, pk = (pA, "pA") if h % 2 == 0 else (pB, "pB")
                    for cc in range(4):
                        for kc in range(8):
                            MM(pb[:, cc * 128:(cc + 1) * 128], Wzv[:, kc, cc * 128:(cc + 1) * 128], xT[:, kc, tok],
                               start=(kc == 0), stop=(kc == 7), r=[wzk, xk], w=[pk])

                def scan(h):
                    hb = h % 2
                    hv = slice(h * 512, (h + 1) * 512)
                    for kc in range(4):
                        MM(pN[:], qs[hb][:, kc, :], CTb[hb][:, kc, :], start=(kc == 0), stop=False,
                           r=[("qs", hb), ("CTb", 0)], w=["pN"])
                    MM(pN[:], scW[hb][:], vtok[:, hv], start=False, stop=True, r=[("scW", hb), ("vtok", h)], w=["pN"])
                    for kc in range(4):
                        MM(p_den, qs[hb][:, kc, :], nbb[:, h, kc, :], start=(kc == 0), stop=False,
                           r=[("qs", hb), "nbb"], w=["pS"])
                    MM(p_den, scW[hb][:], onesb[:], start=False, stop=True, r=[("scW", hb), "onesb"], w=["pS"])
                    for kc in range(4):
                        MM(p_nup[:, 2 * kc:2 * kc + 2], kw[hb][:, kc * 128:(kc + 1) * 128], onesb[:],
                           r=[("kw", hb), "onesb"], w=["pS"])
                    decay = Abc[:, h * 128 + 127:h * 128 + 128]
                    for kc in range(4):
                        pu, puk = (pU[0], "pU0") if kc % 2 == 0 else (pBu, "pBu")
                        MM(pu[:], kw[hb][:, kc * 128:(kc + 1) * 128], vtok[:, hv], r=[("kw", hb), ("vtok", h)], w=[puk])
                        STT("dve", CT[:, h, kc, :], CT[:, h, kc, :], decay, pu[:], ALU.mult, ALU.add,
                            r=[puk, "Abc", ("CT", h)], w=[("CT", h)])
                    ACT(sm[:, hb:hb + 1], p_den[:, 0:1], AF.Abs, r=["pS"], w=[("sm_r", hb)])
                    TS("dve", sm[:, hb:hb + 1], sm[:, hb:hb + 1], 1.0, None, ALU.max, r=[("sm_r", hb)], w=[("sm_r", hb)])
                    sc.add("dve", lambda e, hb=hb: e.reciprocal(out=sm[:, hb:hb + 1], in_=sm[:, hb:hb + 1]),
                           reads=[("sm_r", hb)], writes=[("sm_r", hb)])
                    STT("dve", nst[:, h, :], nst[:, h, :], decay, p_nup.rearrange("p (k t) -> p k t", t=2)[:, :, 0], ALU.mult, ALU.add,
                        r=["pS", "Abc", "nst"], w=["nst"])
                    CP("pool", nbb[:, h, :, :], nst[:, h, :].unsqueeze(2).to_broadcast([128, 4, 2]), r=["nst"], w=["nbb"])

                def epiA(h):
                    hb = h % 2
                    ACT(sigo[:], pO[:], AF.Sigmoid, r=["pO"], w=["sigo"])
                    PRELOAD(AF.Sqrt)
                    STT("dve", hg[:], pN[:], sm[:, hb:hb + 1], sigo[:], ALU.mult, ALU.mult,
                        r=["pN", ("sm_r", hb), "sigo"], w=["hg"])
                    sc.add("dve", lambda e: e.bn_stats(out=bnst[:], in_=hg[:]), reads=["hg"], writes=["bnst"])
                    sc.add("dve", lambda e: e.bn_aggr(out=sm[:, 4:6], in_=bnst[:]), reads=["bnst"], writes=["sm_mv"])
                    ACT(sm[:, 6:7], sm[:, 5:6], AF.Sqrt, bias=epsb[:, 0:1], r=["sm_mv", "epsb"], w=["sm_rstd"])
                    sc.add("dve", lambda e: e.reciprocal(out=sm[:, 6:7], in_=sm[:, 6:7]), r=["sm_rstd"], w=["sm_rstd"])
                    STT("dve", sm[:, 7:8], sm[:, 4:5], -1.0, sm[:, 6:7], ALU.mult, ALU.mult,
                        r=["sm_mv", "sm_rstd"], w=["sm_nmr"])
                    ACT(hn[:], hg[:], AF.Identity, bias=sm[:, 7:8], scale=sm[:, 6:7],
                        r=["hg", "sm_rstd", "sm_nmr"], w=["hn"])
                    PRELOAD(AF.Silu)

                def epiB(h):
                    for vc in range(4):
                        TR(p_hT[:, vc * 128:(vc + 1) * 128], hn[:, vc * 128:(vc + 1) * 128], identb[:],
                           r=["hn", "identb"], w=["pS"])
                    pb, pk = (pA, "pA") if h % 2 == 0 else (pB, "pB")
                    ACT(sigo[:], pb[:], AF.Silu, r=[pk], w=["sigo"])
                    PRELOAD(AF.Exp)
                    for vc in range(4):
                        oc = 4 * h + vc
                        STT("dve", tmpy[:, vc, :], p_hT[:, vc * 128:(vc + 1) * 128], gng[:, oc:oc + 1], xc32[:, oc, :],
                            ALU.mult, ALU.add, r=["pS", "avec", ("xc32", oc)], w=["hg"])
                    TT("pool", yT[:, 4 * h:4 * h + 4, :], tmpy, sz, ALU.mult, r=["hg", "sigo"], w=[("yT", h)])

                proj(0)
                prep(0)
                for h in range(HA):
                    scan(h)
                    if h + 1 < HA:
                        prep(h + 1)
                    epiA(h)
                    if h + 1 < HA:
                        proj(h + 1)
                    epiB(h)
                if DBG == 7:
                    continue
                ally = [("yT", h) for h in range(HA)]
                for j in range(4):
                    W, wk_ = next_block()
                    Wv = W[:].rearrange("p (k c) -> p k c", k=16)
                    pb, pk = (pA, "pA") if j < 2 else (pB, "pB")
                    for i in range(2):
                        dc = 2 * j + i
                        col = (dc % 4) * 128
                        for cc in range(NIC):
                            MM(pb[:, col:col + 128], Wv[:, cc, i * 128:(i + 1) * 128], yT[:, cc, :],
                               start=(cc == 0), stop=(cc == NIC - 1), r=[wk_] + ally, w=[pk])
                    if j % 2 == 1:
                        d0 = (j // 2) * 4
                        STT("dve", vbuf[:, d0:d0 + 4, :], xT[:, d0:d0 + 4, tok], ALPHA,
                            pb[:].rearrange("p (c t) -> p c t", c=4), ALU.mult, ALU.add, r=[pk, xk], w=["vbuf"])
                ln_tail(t, l, last_layer)

        if n_layers > 2:
            sc.barrier(lambda e: e.memset(sm[:, 0:1], 0.0))
            GROUPS = ((128, 1), (512, 4), (2048, 16))
            slopes = [2.0 ** (-8.0 * (i + 1.0) / 8.0) for i in range(8)]
            OD = CT[:].rearrange("p h k v -> p (h k v)").rearrange("p (a t) -> p a t", a=2)
            QTs = [xm32[:].rearrange("p c t -> p (c t)")[:, 0:2048].bitcast(BF16),
                   xmcb[:].rearrange("p c t -> p (c t)")]
            KTs = [xc32[:].rearrange("p c t -> p (c t)").bitcast(BF16), qkT[:].rearrange("p c t -> p (c t)")]
            Vts = [bd[:].rearrange("p a c m -> p (a c m)")[:, 0:4096].rearrange("p (b d) -> p b d", d=128),
                   ktvt[:].rearrange("p (b d) -> p b d", d=128)]
            sqf = sq[:].rearrange("p c t -> p (c t)")
            tabs = [hg[:, 0:256].rearrange("p (a q) -> p a q", a=2), sqf[:, 0:256].rearrange("p (a q) -> p a q", a=2)]
            s_bufs = [sigo[:, 0:256].rearrange("p (a q) -> p a q", a=2),
                      sigo[:, 256:512].rearrange("p (a q) -> p a q", a=2),
                      hg[:, 256:512].rearrange("p (a q) -> p a q", a=2)]
            e_bufs = [hn[:, 0:256].rearrange("p (a q) -> p a q", a=2),
                      hn[:, 256:512].rearrange("p (a q) -> p a q", a=2),
                      kw[0][:, 0:256].rearrange("p (a q) -> p a q", a=2)]
            ctf = CTb0[:].rearrange("p k v -> p (k v)").bitcast(F32)
            JMf = ctf[:, 0:512]
            JM = JMf.rearrange("p (a q) -> p a q", a=4)
            recb = ctf[:, 512:1024]
            szb = Abc[:]
            ones128 = qs[0][:, 0, :]
            st0 = kw[1][:]
            st1 = qs[1][:].rearrange("p c t -> p (c t)")
            DMA("sp", JMf, jm_d[:, :], "const2", w=["JM"])
            sc.add("pool", lambda e: e.memset(ones128, 1.0), w=["ones128"])
            nlb = n_layers - 2
            wkvv = b_wkv_b.rearrange("(kc p) c -> p kc c", p=128)
            evi = 0
            kvst = [(st0, "st0"), (st1, "st1"), (hn[:], "hn"), (sigo[:].bitcast(BF16)[:, 0:512], "sigo")]
            kvps = [(pA, "pA"), (pB, "pB"), (pO, "pO"), (pN, "pN")]
            for j in range(12):
                slot = j % 3
                Wv = ring[slot][:].rearrange("p (k c) -> p k c", k=8)
                DMA("sp", Wv, wkvv[:, :, j * 512:(j + 1) * 512], ("ring", slot), r=[("wkvb", j)], w=[("ring", slot)])
                g, isv, half = j // 4, (j // 2) % 2, j % 2
                for tt in range(S // 512):
                    t5 = slice(tt * 512, (tt + 1) * 512)
                    xks = [("xT", 4 * tt + i) for i in range(4)]
                    for q4 in range(4):
                        pb, pk = kvps[evi % 4]
                        stg, sk = kvst[evi % 4]
                        if not isv:
                            for kc in range(8):
                                MM(pb[:], Wv[:, kc, q4 * 128:(q4 + 1) * 128], xT[:, kc, t5], start=(kc == 0),
                                   stop=(kc == 7), r=[("ring", slot)] + xks, w=[pk])
                            CP("act" if evi % 2 == 0 else "dve", stg, pb[:], r=[pk], w=[sk])
                            DMA("sp", kt_d[g * 8 + half * 4 + q4, :, t5], stg, ("kvst", evi % 4), r=[sk], w=["kt_d"])
                        else:
                            tk = slice(tt * 512 + q4 * 128, tt * 512 + (q4 + 1) * 128)
                            for kc in range(8):
                                MM(pb[:], xT[:, kc, tk], Wv[:, kc, :], start=(kc == 0), stop=(kc == 7),
                                   r=[("ring", slot)] + xks, w=[pk])
                            CP("act" if evi % 2 == 0 else "dve", stg, pb[:], r=[pk], w=[sk])
                            DMA("sp", v_d[g, tk, half * 512:(half + 1) * 512], stg, ("kvst", evi % 4), r=[sk], w=["v_d"])
                        evi += 1
            allx = [("xT", i) for i in range(NT)]
            for lb in range(nlb):
                l = 2 + lb
                last_layer = (l == n_layers - 1)
                winv = b_win_b[lb].rearrange("(kc p) c -> p kc c", p=128)
                woutv = b_wout_b[lb].rearrange("(kc p) c -> p kc c", p=128)
                seq = [(h, g) for h in range(8) for g in range(3)]

                def load(n, lb=lb, winv=winv):
                    h, g = seq[n]
                    dil = GROUPS[g][1]
                    NB = 32 // dil
                    p = n % 2
                    rs = 0 if p == 0 else 2
                    Wq = ring[rs][:, 0:1024].rearrange("p (k c) -> p k c", k=8)
                    DMA("sp", Wq, winv[:, :, g * 1024 + h * 128:g * 1024 + (h + 1) * 128], ("ring", rs),
                        r=[("bwinb", lb, (g * 1024 + h * 128) // 512)], w=[("ring", rs)])
                    DMA("sp", KTs[p], kt_d[g * 8 + h, :, :], ("ktl", p), r=["kt_d"], w=[("KT", p)])
                    Vt = Vts[p]
                    if dil == 1:
                        for b0 in range(0, 32, 8):
                            vsrc = bass.AP(v_d.tensor, g * S * D + h * 128 + b0 * 128 * D,
                                           [[D, 128], [128 * D, 8], [1, 128]])
                            DMA("sp", Vt[:, b0:b0 + 8, :], vsrc, ("vtl", p), r=["v_d"], w=[("Vt", p)])
                    else:
                        for r0 in range(dil):
                            vsrc = bass.AP(v_d.tensor, g * S * D + h * 128 + r0 * D,
                                           [[dil * D, 128], [128 * dil * D, NB], [1, 128]])
                            DMA("sp", Vt[:, r0 * NB:(r0 + 1) * NB, :], vsrc, ("vtl", p), r=["v_d"], w=[("Vt", p)])
                    STT("dve", tabs[p], JM[:, 0:2, :], -slopes[h] * dil, JM[:, 2:4, :], ALU.mult, ALU.add,
                        r=["JM"], w=[("tab", p), ("sq" if p == 1 else "hg")])
                    if g == 1:
                        Wz = ring[1][:, 0:1024].rearrange("p (k c) -> p k c", k=8)
                        DMA("sp", Wz, winv[:, :, 3 * 1024 + h * 128:3 * 1024 + (h + 1) * 128], ("ring", 1),
                            r=[("bwinb", lb, (3 * 1024 + h * 128) // 512)], w=[("ring", 1)])

                def qproj_tile(n, tt):
                    p = n % 2
                    rs = 0 if p == 0 else 2
                    Wq = ring[rs][:, 0:1024].rearrange("p (k c) -> p k c", k=8)
                    pb, pk = (pA, "pA") if tt % 2 == 0 else (pB, "pB")
                    for kc in range(8):
                        MM(pb[:], Wq[:, kc, :], xT[:, kc, tt * 512:(tt + 1) * 512], start=(kc == 0),
                           stop=(kc == 7), r=[("ring", rs)] + allx, w=[pk])
                    ACT(QTs[p][:, tt * 512:(tt + 1) * 512], pb[:], AF.Copy, scale=128.0 ** -0.5, r=[pk], w=[("QT", p)])

                ps1s = ((pO, "pO"), (pN, "pN"), (pU[0], "pU0"))
                ps2s = ((pBu, "pBu"), (pBm, "pBm"), (pS, "pS"))

                def stage1(n, ui):
                    h, g = seq[n]
                    dil = GROUPS[g][1]
                    NB = 32 // dil
                    p = n % 2
                    QT, KT, tab = QTs[p], KTs[p], tabs[p]
                    r_, nb = ui // NB, ui % NB
                    c0 = r_ + dil * 128 * nb
                    qc = QT[:, c0:c0 + dil * 127 + 1:dil]
                    na = 2 if nb > 0 else 1
                    s_sb, e_sb = s_bufs[ui % 3], e_bufs[ui % 3]
                    sk_, ek_ = ("s_sb", ui % 3), ("e_sb", ui % 3)
                    ps1, pk1 = ps1s[ui % 3]
                    MM(ps1[:, 0:128], KT[:, c0:c0 + dil * 127 + 1:dil], qc, r=[("KT", p), ("QT", p)], w=[pk1])
                    if nb > 0:
                        cp_ = c0 - dil * 128
                        MM(ps1[:, 128:256], KT[:, cp_:cp_ + dil * 127 + 1:dil], qc, r=[("KT", p), ("QT", p)], w=[pk1])
                    TT("dve", s_sb[:, 0:na, :], ps1[:, 0:na * 128].rearrange("p (a q) -> p a q", a=na),
                       tab[:, 0:na, :], ALU.add, r=[pk1, ("tab", p)], w=[sk_])
                    ACT(e_sb[:, 0:na, :], s_sb[:, 0:na, :], AF.Exp, r=[sk_], w=[ek_])

                def stage2(n, ui):
                    h, g = seq[n]
                    dil = GROUPS[g][1]
                    NB = 32 // dil
                    p = n % 2
                    Vt = Vts[p]
                    r_, nb = ui // NB, ui % NB
                    c0 = r_ + dil * 128 * nb
                    e_sb = e_bufs[ui % 3]
                    ek_ = ("e_sb", ui % 3)
                    ps2, pk2 = ps2s[ui % 3]
                    blk = r_ * NB + nb
                    MM(ps2[:, 0:128], Vt[:, blk, :], e_sb[:, 0, :], start=True, stop=(nb == 0),
                       r=[("Vt", p), ek_], w=[pk2])
                    if nb > 0:
                        MM(ps2[:, 0:128], Vt[:, blk - 1, :], e_sb[:, 1, :], start=False, stop=True,
                           r=[("Vt", p), ek_], w=[pk2])
                    MM(ps2[:, 128:256], ones128, e_sb[:, 0, :], start=True, stop=(nb == 0),
                       r=["ones128", ek_], w=[pk2])
                    if nb > 0:
                        MM(ps2[:, 128:256], ones128, e_sb[:, 1, :], start=False, stop=True,
                           r=["ones128", ek_], w=[pk2])
                    odv = OD[:, :, c0:c0 + dil * 127 + 1:dil]
                    pin = ps2[:, 0:256].rearrange("p (a q) -> p a q", a=2)
                    if g == 0:
                        odk = [("OD", nb // 4)]
                    elif g == 1:
                        odk = [("OD", nb)]
                    else:
                        odk = [("OD", 4 * nb + i_) for i_ in range(4)]
                    if g == 0:
                        CP("act", odv, pin, r=[pk2], w=odk)
                    else:
                        TT("dve", odv, odv, pin, ALU.add, r=[pk2] + odk, w=odk)

                def gating(h):
                    Wz = ring[1][:, 0:1024].rearrange("p (k c) -> p k c", k=8)
                    for tt in range(8):
                        t5 = slice(tt * 512, (tt + 1) * 512)
                        pb, pk = (pA, "pA") if tt % 2 == 0 else (pB, "pB")
                        for kc in range(8):
                            MM(pb[:], Wz[:, kc, :], xT[:, kc, t5], start=(kc == 0), stop=(kc == 7),
                               r=[("ring", 1)] + allx, w=[pk])
                        vfl = vbuf[:].rearrange("p c t -> p (c t)")
                        rb_, rk_ = (recb, "recb") if tt % 2 == 0 else (vfl[:, 0:512], "recb2")
                        sz_, zk_ = (szb, "szb") if tt % 2 == 0 else (vfl[:, 512:1024], "szb2")
                        ACT(sz_, pb[:], AF.Silu, r=[pk], w=[zk_])
                        sc.add("dve", lambda e, t5=t5, rb_=rb_: e.reciprocal(out=rb_, in_=OD[:, 1, t5]),
                               r=[("OD", tt)], w=[rk_])
                        TT("pool", rb_, rb_, OD[:, 0, t5], ALU.mult, r=[rk_, ("OD", tt)], w=[rk_])
                        stg, sk = (st0, "st0") if tt % 2 == 0 else (st1, "st1")
                        TT("pool", stg, rb_, sz_, ALU.mult, r=[rk_, zk_], w=[sk])
                        DMA("sp", y_d[h, :, t5], stg, ("yst", tt % 2), r=[sk], w=["y_d"])

                load(0)
                for tt in range(8):
                    qproj_tile(0, tt)
                for n in range(len(seq)):
                    h, g = seq[n]
                    if n + 1 < len(seq):
                        load(n + 1)
                    for i in range(32 + 2):
                        if i < 32:
                            stage1(n, i)
                        if i >= 2:
                            stage2(n, i - 2)
                        if n + 1 < len(seq) and i % 4 == 3 and i < 32:
                            qproj_tile(n + 1, i // 4)
                    if g == 2:
                        gating(h)
                Wo0 = ring[0][:].rearrange("p (k c) -> p k c", k=8)
                Wo1 = ring[1][:].rearrange("p (k c) -> p k c", k=8)
                DMA("sp", Wo0, woutv[:, :, 0:512], ("ring", 0), r=[("bwoutb", lb, 0)], w=[("ring", 0)])
                DMA("sp", Wo1, woutv[:, :, 512:1024], ("ring", 1), r=[("bwoutb", lb, 1)], w=[("ring", 1)])
                xmf = xm32[:].rearrange("p c t -> p (c t)")
                vbs = [vbuf, xmf[:, 0:1024].rearrange("p (c t) -> p c t", c=NCH)]
                sqs = [sq, xmf[:, 1024:2048].rearrange("p (c t) -> p c t", c=NCH)]

                def p7(t, lb=lb):
                    par = t % 2
                    tok = slice(t * T, (t + 1) * T)
                    xk = ("xT", t)
                    yk = ("yTf", par)
                    yv = yT[:, 8 * par:8 * par + 8, :]
                    DMA("sp", yv, y_d[:, :, tok].rearrange("h p t -> p h t"), ("ytl", par), r=["y_d"], w=[yk])
                    for dc in range(NCH):
                        pb, pk = (pA, "pA") if dc < 4 else (pB, "pB")
                        Wo, wk_ = (Wo0, ("ring", 0)) if dc < 4 else (Wo1, ("ring", 1))
                        col = (dc % 4) * 128
                        for hh in range(8):
                            MM(pb[:, col:col + 128], Wo[:, hh, col:col + 128], yv[:, hh, :], start=(hh == 0),
                               stop=(hh == 7), r=[wk_, yk], w=[pk])
                        if dc % 4 == 3:
                            d0 = (dc // 4) * 4
                            STT("dve", vbs[par][:, d0:d0 + 4, :], xT[:, d0:d0 + 4, tok], ALPHA,
                                pb[:].rearrange("p (c t) -> p c t", c=4), ALU.mult, ALU.add, r=[pk, xk],
                                w=[("vbf", par)])

                p7(0)
                for t in range(NT):
                    if t + 1 < NT:
                        p7(t + 1)
                    ln_tail(t, l, last_layer, vb=vbs[t % 2], sqb=sqs[t % 2], vk=("vbf", t % 2), sk=("sqf", t % 2))
        counts = sc.emit(nc, st, final_wait_ops=final_ops[-1:] if final_ops else [])
        print("op counts", counts, flush=True)
    return nc


final_ops = []
import os
DBG = int(os.environ.get("KDBG", "0"))


def _prep(inputs):
    f = np.float32
    ident = np.eye(128, dtype=f)
    utri = np.triu(np.ones((128, 128), f))
    mneg = np.where(utri > 0, 0.0, -30000.0).astype(f)
    ones = np.ones((128, 128), f)
    cst = np.stack([ident, utri, mneg, ones, ones, ones], axis=1)
    shared = {"cst": np.ascontiguousarray(cst)}
    for l in range(2):
        shared["a_win%d" % l] = np.ascontiguousarray(inputs["a_w_in"][l], dtype=f)
        shared["a_wout%d" % l] = np.ascontiguousarray(inputs["a_w_out"][l], dtype=f)
        bds = []
        for nm in ("a_wq", "a_wk", "a_wv"):
            w = np.asarray(inputs[nm][l], dtype=f)
            full = np.zeros((NIC, 128, 128), f)
            wr = w.reshape(NIC, 32, 4, 4)
            for n in range(32):
                full[:, n * 4:(n + 1) * 4, n * 4:(n + 1) * 4] = np.transpose(wr[:, n], (0, 2, 1))
            bds.append(np.transpose(full, (1, 0, 2)))
        shared["a_bd%d" % l] = np.ascontiguousarray(np.stack(bds, axis=1).reshape(128, -1))
        wif = np.asarray(inputs["a_w_if"][l], dtype=f).reshape(48, 128, 8).transpose(1, 0, 2)
        shared["a_wif%d" % l] = np.ascontiguousarray(wif.reshape(128, -1))
        bif = np.broadcast_to(np.asarray(inputs["a_b_if"][l], dtype=f)[None, :], (128, 8))
        cwv = np.asarray(inputs["a_conv_w"][l], dtype=f).reshape(4, NIC, 128).transpose(2, 1, 0).reshape(128, 64)
        cbv = np.asarray(inputs["a_conv_b"][l], dtype=f).reshape(NIC, 128).T
        gng = np.asarray(inputs["a_gn_g"][l], dtype=f).reshape(NIC, 128).T
        skp = np.asarray(inputs["a_skip"][l], dtype=f).reshape(NIC, 128).T
        shared["a_vec%d" % l] = np.ascontiguousarray(np.concatenate([bif, cwv, cbv, gng, skp], axis=1))
    kk = np.arange(128)[:, None]
    qq = np.arange(128)[None, :]
    jcur = np.where(qq >= kk, qq - kk, 0).astype(f)
    mcur = np.where(qq >= kk, 0.0, -30000.0).astype(f)
    jprev = np.where(kk >= qq, qq + 128 - kk, 0).astype(f)
    mprev = np.where(kk >= qq, 0.0, -30000.0).astype(f)
    shared["jm"] = np.ascontiguousarray(np.stack([jcur, jprev, mcur, mprev], axis=1).reshape(128, 512))
    shared["b_wkv"] = np.ascontiguousarray(inputs["b_w_kv"], dtype=f)
    for l in range(2):
        shared["b_win%d" % l] = np.ascontiguousarray(inputs["b_w_in"][l], dtype=f)
        shared["b_wout%d" % l] = np.ascontiguousarray(inputs["b_w_out"][l], dtype=f)
    lng = np.asarray(inputs["ln_g"], dtype=f).reshape(DEPTH, NCH, 128)
    lnb = np.asarray(inputs["ln_b"], dtype=f).reshape(DEPTH, NCH, 128)
    lnv = np.stack([lng, lnb], axis=1)
    shared["lnv"] = np.ascontiguousarray(lnv.transpose(3, 0, 1, 2).reshape(128, -1))
    return shared


def kernel(_n_layers=4, _n_tiles=NT, **inputs):
    x = np.ascontiguousarray(inputs["x"], dtype=np.float32)
    del final_ops[:]
    nc = build(_n_layers, _n_tiles)
    shared = _prep(inputs)
    in_maps = []
    for c in range(8):
        m = dict(shared)
        m["x"] = x[c] if c < 4 else np.zeros_like(x[0])
        in_maps.append(m)
    res = run_bass_kernel_spmd(nc, in_maps, core_ids=list(range(8)))
    return np.stack([res.results[c]["out"] for c in range(4)], axis=0)
```
